# Optimizing a Trainium2 kernel written in Bass

```python
import jax, jax.numpy as jnp
from jax import lax
import numpy as np

D_MODEL = 1024
BATCH = 8
SEQ = 8192
DEPTH = 1

PLE_DIM = 256
ATT_HEADS = 8
ATT_KV_HEADS = 2
HEAD_DIM = 64
WINDOW = 128
BLOCK = 128
D_ATT = ATT_HEADS * HEAD_DIM
D_KV = ATT_KV_HEADS * HEAD_DIM
D_RNN = D_MODEL - D_ATT
RNN_BLOCKS = 8
RNN_BLOCK_DIM = D_RNN // RNN_BLOCKS
RNN_CONV = 4
LRU_C = 8.0
D_MIX = D_ATT + D_RNN
D_IN = D_ATT + 2 * D_KV + 2 * D_RNN
D_FF = 3 * D_MODEL
FFN_CONV = 3
LN_EPS = 1e-5
ALPHA = float((2 * DEPTH) ** 0.25)
BETA = float((8 * DEPTH) ** -0.25)

kernel_name = "hymba_swa_sink_rglru_convglu_deepnorm"


def layer_norm(x, g, b):
    xf = x.astype(jnp.float32)
    mu = jnp.mean(xf, axis=-1, keepdims=True)
    var = jnp.mean(jnp.square(xf - mu), axis=-1, keepdims=True)
    y = (xf - mu) * lax.rsqrt(var + LN_EPS)
    return (y * g.astype(jnp.float32) + b.astype(jnp.float32)).astype(x.dtype)


def causal_dwconv(x, w, b):
    width = w.shape[0]
    y = lax.conv_general_dilated(
        x, w[:, None, :].astype(x.dtype), window_strides=(1,),
        padding=[(width - 1, 0)], dimension_numbers=('NWC', 'WIO', 'NWC'),
        feature_group_count=x.shape[-1])
    return y + b.astype(x.dtype)


def sliding_window_sink_attention(q, k, v, sinks):
    B, S = q.shape[0], q.shape[1]
    nb = S // BLOCK
    grp = ATT_HEADS // ATT_KV_HEADS
    qb = q.reshape(B, nb, BLOCK, ATT_KV_HEADS, grp, HEAD_DIM).astype(jnp.float32)

    def band(t):
        tb = t.reshape(B, nb, BLOCK, ATT_KV_HEADS, HEAD_DIM).astype(jnp.float32)
        prev = jnp.pad(tb, ((0, 0), (1, 0), (0, 0), (0, 0), (0, 0)))[:, :-1]
        return jnp.concatenate([prev, tb], axis=2)

    kb, vb = band(k), band(v)
    scores = jnp.einsum('bnqkgd,bnskd->bnkgqs', qb, kb) * (HEAD_DIM ** -0.5)
    qi = jnp.arange(BLOCK)[:, None]
    sj = jnp.arange(2 * BLOCK)[None, :]
    rel = qi + BLOCK - sj
    in_win = (rel >= 0) & (rel < WINDOW)
    key_ok = (jnp.arange(nb)[:, None] * BLOCK - BLOCK + sj) >= 0
    mask = in_win[None] & key_ok[:, None, :]
    scores = jnp.where(mask[None, :, None, None], scores, -jnp.inf)
    sink = sinks.astype(jnp.float32).reshape(ATT_KV_HEADS, grp)[None, None, :, :, None, None]
    sink = jnp.broadcast_to(sink, scores.shape[:-1] + (1,))
    probs = jax.nn.softmax(jnp.concatenate([scores, sink], axis=-1), axis=-1)[..., :-1]
    out = jnp.einsum('bnkgqs,bnskd->bnqkgd', probs, vb)
    return out.reshape(B, S, D_ATT).astype(q.dtype)


def rg_lru(x, w_a, b_a, w_x, b_x, lam):
    B, S, _ = x.shape
    xf = x.astype(jnp.float32)
    xb = xf.reshape(B, S, RNN_BLOCKS, RNN_BLOCK_DIM)
    r = jax.nn.sigmoid(jnp.einsum('bshi,hij->bshj', xb, w_a.astype(jnp.float32)).reshape(B, S, D_RNN)
                       + b_a.astype(jnp.float32))
    i = jax.nn.sigmoid(jnp.einsum('bshi,hij->bshj', xb, w_x.astype(jnp.float32)).reshape(B, S, D_RNN)
                       + b_x.astype(jnp.float32))
    log_a = -LRU_C * r * jax.nn.softplus(-lam.astype(jnp.float32))
    a = jnp.exp(log_a)
    b = jnp.sqrt(-jnp.expm1(2.0 * log_a)) * (i * xf)

    def combine(c1, c2):
        a1, b1 = c1
        a2, b2 = c2
        return a1 * a2, a2 * b1 + b2

    _, h = lax.associative_scan(combine, (a, b), axis=1)
    return h.astype(x.dtype)


def conv_glu_ffn(h, w_up, conv_w, conv_b, w_down):
    up = h @ w_up
    gate, val = jnp.split(up, 2, axis=-1)
    gate = causal_dwconv(gate, conv_w, conv_b)
    return (jax.nn.gelu(gate, approximate=True) * val) @ w_down


def setup_inputs(seed: int = 0) -> dict:
    key = jax.random.key(seed)
    ks = jax.random.split(key, 24)
    f32 = jnp.float32
    nrm = lambda k, shape, s: jax.random.normal(k, shape, f32) * s
    L = DEPTH
    u = jax.random.uniform(ks[12], (L, D_RNN), f32, 0.9, 0.999)
    s = u ** (1.0 / LRU_C)
    lru_lambda = jnp.log(s) - jnp.log1p(-s)
    return {
        "x": nrm(ks[0], (BATCH, SEQ, D_MODEL), 1.0),
        "p": nrm(ks[1], (DEPTH, BATCH, SEQ, PLE_DIM), 1.0),
        "w_in": nrm(ks[2], (L, D_MODEL, D_IN), D_MODEL ** -0.5),
        "attn_sinks": nrm(ks[3], (L, ATT_HEADS), 0.5),
        "rnn_conv_w": nrm(ks[4], (L, RNN_CONV, D_RNN), RNN_CONV ** -0.5),
        "rnn_conv_b": nrm(ks[5], (L, D_RNN), 0.01),
        "gate_a_w": nrm(ks[6], (L, RNN_BLOCKS, RNN_BLOCK_DIM, RNN_BLOCK_DIM), RNN_BLOCK_DIM ** -0.5),
        "gate_a_b": nrm(ks[7], (L, D_RNN), 0.01),
        "gate_x_w": nrm(ks[8], (L, RNN_BLOCKS, RNN_BLOCK_DIM, RNN_BLOCK_DIM), RNN_BLOCK_DIM ** -0.5),
        "gate_x_b": nrm(ks[9], (L, D_RNN), 0.01),
        "lru_lambda": lru_lambda,
        "w_out": nrm(ks[10], (L, D_MIX, D_MODEL), BETA * D_MIX ** -0.5),
        "ln1_g": 1.0 + nrm(ks[11], (L, D_MODEL), 0.01),
        "ln1_b": nrm(ks[13], (L, D_MODEL), 0.01),
        "w_ffn_up": nrm(ks[14], (L, D_MODEL, 2 * D_FF), D_MODEL ** -0.5),
        "ffn_conv_w": nrm(ks[15], (L, FFN_CONV, D_FF), FFN_CONV ** -0.5),
        "ffn_conv_b": nrm(ks[16], (L, D_FF), 0.01),
        "w_ffn_down": nrm(ks[17], (L, D_FF, D_MODEL), BETA * D_FF ** -0.5),
        "ple_gate_w": nrm(ks[18], (L, D_MODEL, D_MODEL), D_MODEL ** -0.5),
        "ple_gate_b": nrm(ks[19], (L, D_MODEL), 0.01),
        "ple_proj": nrm(ks[20], (L, PLE_DIM, D_MODEL), BETA * PLE_DIM ** -0.5),
        "ln2_g": 1.0 + nrm(ks[21], (L, D_MODEL), 0.01),
        "ln2_b": nrm(ks[22], (L, D_MODEL), 0.01),
    }


def reference(x, p, w_in, attn_sinks, rnn_conv_w, rnn_conv_b, gate_a_w, gate_a_b,
              gate_x_w, gate_x_b, lru_lambda, w_out, ln1_g, ln1_b, w_ffn_up,
              ffn_conv_w, ffn_conv_b, w_ffn_down, ple_gate_w, ple_gate_b, ple_proj,
              ln2_g, ln2_b):
    B, S, _ = x.shape
    splits = [D_ATT, D_ATT + D_KV, D_ATT + 2 * D_KV, D_ATT + 2 * D_KV + D_RNN]
    h = x
    for l in range(DEPTH):
        u = h @ w_in[l]
        q, k, v, xr, gr = jnp.split(u, splits, axis=-1)
        att = sliding_window_sink_attention(
            q.reshape(B, S, ATT_HEADS, HEAD_DIM),
            k.reshape(B, S, ATT_KV_HEADS, HEAD_DIM),
            v.reshape(B, S, ATT_KV_HEADS, HEAD_DIM),
            attn_sinks[l])
        xr = causal_dwconv(xr, rnn_conv_w[l], rnn_conv_b[l])
        rec = rg_lru(xr, gate_a_w[l], gate_a_b[l], gate_x_w[l], gate_x_b[l], lru_lambda[l])
        rec = rec * jax.nn.gelu(gr, approximate=True)
        mix = jnp.concatenate([att, rec], axis=-1) @ w_out[l]
        h = layer_norm(ALPHA * h + mix, ln1_g[l], ln1_b[l])
        ffn = conv_glu_ffn(h, w_ffn_up[l], ffn_conv_w[l], ffn_conv_b[l], w_ffn_down[l])
        ple = jax.nn.sigmoid(h @ ple_gate_w[l] + ple_gate_b[l]) * (p[l] @ ple_proj[l])
        h = layer_norm(ALPHA * h + ffn + ple, ln2_g[l], ln2_b[l])
    return h
```

```python
from contextlib import ExitStack
import numpy as np
import concourse.bass as bass
import concourse.mybir as mybir
from concourse.bass_utils import run_bass_kernel_spmd

F32 = mybir.dt.float32
BF16 = mybir.dt.bfloat16
AF = mybir.ActivationFunctionType
ALU = mybir.AluOpType

D = 1024
SEQ = 8192
NB = 8
T = 512
ALPHA = float(2.0 ** 0.25)
EPS = 1e-5
R1 = 2
R2 = 4
NP1 = 8
NP2 = 45
NPV = 172
ENGS = ("pe", "act", "dve", "pool", "sp")


class Buf:
    __slots__ = ("name", "w", "r", "excl")

    def __init__(self, name, excl=False):
        self.name = name
        self.w = None
        self.r = []
        self.excl = excl


class Ev:
    __slots__ = ("key", "val", "eng", "snap")

    def __init__(self, key, val, eng, snap):
        self.key = key
        self.val = val
        self.eng = eng
        self.snap = snap


class Sched:
    def __init__(self, nc, ctx):
        self.nc = nc
        self.ctx = ctx
        self.ops = {e: [] for e in ENGS}
        self.cnt = {}
        self.seen = {e: {} for e in ENGS}
        self.sems = {}
        self.last = {}
        for e in ENGS:
            if e != "sp":
                self.sems[e] = ctx.enter_context(nc.semaphore("s_" + e))
                self.cnt[e] = 0
        self.n_dma_sem = 0
        self.pending = {e: False for e in ENGS}
        self.total = 0
        self.limit = 1 << 60

    def dma_sem(self):
        self.n_dma_sem += 1
        key = "dma%d" % self.n_dma_sem
        self.sems[key] = self.ctx.enter_context(self.nc.semaphore(key))
        self.cnt[key] = 0
        return key

    def _need(self, eng, reads, writes, extra=()):
        need = {}

        def req(ev, kind):
            if ev is None:
                return
            if ev.key == eng:
                if eng == "pe":
                    return
                if kind == "war":
                    return
            if need.get(ev.key) is None or need[ev.key].val < ev.val:
                need[ev.key] = ev

        for b in reads:
            req(b.w, "raw")
            if b.excl:
                for r in b.r:
                    if r.key != eng:
                        req(r, "raw")
        for b in writes:
            req(b.w, "waw")
            for r in b.r:
                req(r, "war")
        for ev in extra:
            req(ev, "raw")
        seen = self.seen[eng]
        waits = []
        for k, ev in need.items():
            if seen.get(k, 0) < ev.val:
                waits.append((k, ev.val))
        for k, ev in need.items():
            if seen.get(k, 0) < ev.val:
                seen[k] = ev.val
            for k2, v2 in ev.snap.items():
                if seen.get(k2, 0) < v2:
                    seen[k2] = v2
        return waits

    def _commit(self, ev, reads, writes):
        for b in reads:
            b.r.append(ev)
            if len(b.r) > 64:
                best = {}
                for r in b.r:
                    if best.get(r.key) is None or best[r.key].val < r.val:
                        best[r.key] = r
                b.r = list(best.values())
        for b in writes:
            b.w = ev
            b.r = []

    def op(self, eng, fn, reads=(), writes=(), inc=True):
        self.total += 1
        if self.total > self.limit:
            return None
        waits = self._need(eng, reads, writes)
        if inc:
            self.cnt[eng] += 1
            val = self.cnt[eng]
            self.pending[eng] = False
        else:
            val = self.cnt[eng] + 1
            self.pending[eng] = True
        ev = Ev(eng, val, eng, dict(self.seen[eng]))
        self._commit(ev, reads, writes)
        self.ops[eng].append((waits, fn, eng if inc else None, 1))
        return ev

    def dma(self, eng, semkey, fn, reads=(), writes=()):
        self.total += 1
        if self.total > self.limit:
            return None
        extra = (self.last[semkey],) if semkey in self.last else ()
        waits = self._need(eng, reads, writes, extra)
        self.cnt[semkey] += 16
        ev = Ev(semkey, self.cnt[semkey], eng, dict(self.seen[eng]))
        self.last[semkey] = ev
        self._commit(ev, reads, writes)
        self.ops[eng].append((waits, fn, semkey, 16))
        return ev

    def wait_events(self, eng, evs):
        waits = self._need(eng, (), (), evs)
        self.ops[eng].append((waits, None, None, 0))

    def drain(self, eng):
        waits = []
        for k, v in self.cnt.items():
            if v > 0 and self.seen[eng].get(k, 0) < v:
                waits.append((k, v))
                self.seen[eng][k] = v
        self.ops[eng].append((waits, None, None, 0))

    def emit(self):
        nc = self.nc
        sems = self.sems
        ops = self.ops

        def replay(engh, name):
            for waits, fn, semkey, inc in ops[name]:
                for k, v in waits:
                    engh.wait_ge(sems[k], v)
                if fn is None:
                    continue
                ins = fn(engh)
                if semkey is not None:
                    ins.then_inc(sems[semkey], inc)

        with nc.Block() as block:
            @block.tensor
            def _(e):
                replay(e, "pe")

            @block.scalar
            def _(e):
                replay(e, "act")

            @block.vector
            def _(e):
                replay(e, "dve")

            @block.gpsimd
            def _(e):
                replay(e, "pool")

            @block.sync
            def _(e):
                replay(e, "sp")


MARKS = []
_SREF = [None]


def interleave(*gens):
    gens = [g for g in gens if g is not None]
    while gens:
        for g in list(gens):
            try:
                next(g)
                MARKS.append(_SREF[0].total)
            except StopIteration:
                gens.remove(g)


def interleave_pat(ga, gb, pat):
    gens = {"A": ga, "B": gb}
    alive = {k for k, g in gens.items() if g is not None}
    i = 0
    while alive:
        if i < len(pat):
            k = pat[i]
        else:
            k = "AB"[(i - len(pat)) % 2]
        i += 1
        if k not in alive:
            k = next(iter(alive))
        try:
            next(gens[k])
            MARKS.append(_SREF[0].total)
        except StopIteration:
            alive.discard(k)


def take(gen, n):
    for _ in range(n):
        try:
            next(gen)
        except StopIteration:
            return
        yield


def chain(*gens):
    for g in gens:
        if g is not None:
            yield from g


def build(NT, dbg=0, limit=None):
    ntiles = NT // T
    nc = bass.Bass("TRN2", target_bir_lowering=False)
    x_d = nc.dram_tensor("x", [NT, D], F32, kind="ExternalInput").ap()
    p_d = nc.dram_tensor("p", [NT, 256], F32, kind="ExternalInput").ap()
    w1_d = nc.dram_tensor("w1", [NP1, 128, 2048], F32, kind="ExternalInput").ap()
    w2_d = nc.dram_tensor("w2", [NP2, 128, 2048], F32, kind="ExternalInput").ap()
    gw_d = nc.dram_tensor("gw", [128, 1024], F32, kind="ExternalInput").ap()
    mask_d = nc.dram_tensor("mask", [128, 1024], F32, kind="ExternalInput").ap()
    ident_d = nc.dram_tensor("ident", [128, 128], F32, kind="ExternalInput").ap()
    pv_d = nc.dram_tensor("pv", [128, NPV], F32, kind="ExternalInput").ap()
    out_d = nc.dram_tensor("out", [NT, D], F32, kind="ExternalOutput").ap()
    s1_d = nc.dram_tensor("s1", [NP1, 128, 2048], BF16).ap()
    s2_d = nc.dram_tensor("s2", [NP2, 128, 2048], BF16).ap()

    with ExitStack() as ctx:
        S = Sched(nc, ctx)
        _SREF[0] = S
        MARKS.append(-1)
        if limit:
            S.limit = limit

        def sb(name, shape, dt):
            return ctx.enter_context(nc.sbuf_tensor("sb_" + name, shape, dt))

        ident = sb("ident", [128, 128], F32)
        maskt = sb("maskt", [128, 1024], BF16)
        ones64 = sb("ones64", [128, 64], BF16)
        onesm = sb("onesm", [128, 128], BF16)
        pv = sb("pv", [128, NPV], F32)
        dv = sb("dv", [128, 48], F32)
        gw = sb("gw", [128, 1024], BF16)
        ring1 = sb("ring1", [128, R1, 2048], BF16)
        ring2 = sb("ring2", [128, R2, 2048], BF16)
        xraw = sb("xraw", [128, 2, 1024], F32)
        xT = sb("xT", [128, 2, 8, T], F32)
        hT = sb("hT", [128, 8, T], BF16)
        qT = sb("qT", [128, 4, T], BF16)
        kT = sb("kT", [128, 2, 640], BF16)
        Vt = sb("Vt", [128, 640], BF16)
        xr = sb("xr", [128, 4, 516], F32)
        gg = sb("gg", [128, 4, T], F32)
        xc = sb("xc", [128, 2, T], F32)
        xcb = sb("xcb", [128, 2, T], BF16)
        rr = sb("rr", [128, 2, T], F32)
        ii = sb("ii", [128, 2, T], F32)
        aa = sb("aa", [128, 2, T], F32)
        hs = sb("hs", [128, 2, T], F32)
        hst = sb("hst", [128, 4], F32)
        expT = sb("expT", [128, 2, 1024], BF16)
        rden = sb("rden", [128, 2, 256], F32)
        mixT = sb("mixT", [128, 8, T], BF16)
        h1T = sb("h1T", [128, 8, T], BF16)
        zb = sb("zb", [128, 2, T], BF16)
        zq = sb("zq", [128, 2, T], BF16)
        lnm = sb("lnm", [128, 3, T], F32)
        graw = sb("graw", [128, 2, 516], F32)
        yy = sb("yy", [128, 3, T], F32)
        halo = sb("halo", [128, 24, 2], F32)
        actb = sb("actb", [128, 24, T], BF16)
        praw = sb("praw", [128, 4, 256], F32)
        pT = sb("pT", [128, 2, T], BF16)
        sg = sb("sg", [128, 2, T], F32)
        ot = sb("ot", [128, 1, 1024], F32)
        psb = [ctx.enter_context(nc.psum_tensor("ps%d" % i, [128, 512], F32)) for i in range(8)]

        B = {}

        def bf(name):
            if name not in B:
                B[name] = Buf(name)
            return B[name]

        bank = [bf("bank%d" % i) for i in range(8)]
        for b_ in bank:
            b_.excl = True
        free_banks = list(range(8))

        def balloc():
            assert free_banks, "out of PSUM banks"
            return free_banks.pop(0)

        def bfree(i):
            free_banks.append(i)

        def ACT(out, in_, func, reads, writes, bias=None, scale=None):
            kw = {}
            if bias is not None:
                kw["bias"] = bias
            if scale is not None:
                kw["scale"] = scale
            S.op("act", lambda e: e.activation(out, in_, func, **kw), reads, writes)

        def TT(eng, out, in0, in1, op, reads, writes):
            S.op(eng, lambda e: e.tensor_tensor(out, in0, in1, op), reads, writes)

        def STT(eng, out, in0, scalar, in1, op0, op1, reads, writes):
            S.op(eng, lambda e: e.scalar_tensor_tensor(out, in0, scalar, in1, op0, op1), reads, writes)

        def TS(eng, out, in0, s1, s2, op0, op1, reads, writes):
            if op1 is Ellipsis:
                S.op(eng, lambda e: e.tensor_scalar(out, in0, s1, s2, op0), reads, writes)
            else:
                S.op(eng, lambda e: e.tensor_scalar(out, in0, s1, s2, op0, op1), reads, writes)

        def CP(eng, out, in_, reads, writes):
            S.op(eng, lambda e: e.tensor_copy(out, in_), reads, writes)

        def MM(out, lhsT, rhs, start, stop, reads, writes, inc):
            S.op("pe", lambda e: e.matmul(out, lhsT, rhs, start=start, stop=stop), reads, writes, inc=inc)

        def TR(out, in_, reads, writes, inc):
            S.op("pe", lambda e: e.transpose(out, in_, ident[:]), reads + [bf("ident")], writes, inc=inc)

        def MS(eng, ap, val, writes):
            S.op(eng, lambda e: e.memset(ap, val), (), writes)

        sem_r1 = [S.dma_sem() for _ in range(R1)]
        sem_r2 = [S.dma_sem() for _ in range(R2)]
        sem_x = [S.dma_sem() for _ in range(2)]
        sem_p = S.dma_sem()
        sem_o = [S.dma_sem() for _ in range(2)]
        sem_c = [S.dma_sem() for _ in range(4)]

        S.dma("sp", sem_c[0], lambda e: e.dma_start(out=ident[:], in_=ident_d), writes=[bf("ident")])
        S.dma("sp", sem_c[1], lambda e: e.dma_start(out=pv[:], in_=pv_d), writes=[bf("pv")])
        pc1 = [bf("pc1_%d" % i) for i in range(NP1)]
        pc2 = [bf("pc2_%d" % i) for i in range(NP2)]
        sem_si = [S.dma_sem() for _ in range(4)]
        sem_so = [S.dma_sem() for _ in range(6)]
        stage_in = [xT[:, k // 2, (k % 2) * 4:(k % 2) * 4 + 4, :] for k in range(4)]
        stage_out = [actb[:, k * 4:(k + 1) * 4, :] for k in range(6)]
        bsi = [bf("stage_in%d" % k) for k in range(4)]
        bso = [bf("stage_out%d" % k) for k in range(6)]
        S.dma("sp", sem_si[0], lambda e: e.dma_start(out=stage_in[0][:, 0:2, :],
                                                     in_=mask_d.rearrange("p (a b) -> p a b", b=T)), writes=[bsi[0]])
        S.dma("sp", sem_si[1], lambda e: e.dma_start(out=stage_in[1][:, 0:2, :],
                                                     in_=gw_d.rearrange("p (a b) -> p a b", b=T)), writes=[bsi[1]])
        CP("dve", maskt[:, :].rearrange("p (a b) -> p a b", b=T), stage_in[0][:, 0:2, :], [bsi[0]], [bf("mask")])
        CP("dve", gw[:, :].rearrange("p (a b) -> p a b", b=T), stage_in[1][:, 0:2, :], [bsi[1]], [bf("gw")])
        plist = [(w1_d, s1_d, pc1, i) for i in range(NP1)] + [(w2_d, s2_d, pc2, i) for i in range(NP2)]
        if dbg:
            plist = plist[:NP1 + 4]
        castn = [0]

        def cast_piece(n, wd, sd, pcs, i, lazy):
            ki = (2 + n % 2) if lazy else (n % 2)
            ko = n % 6
            S.dma("sp", sem_si[ki], lambda e: e.dma_start(
                out=stage_in[ki], in_=wd[i].rearrange("p (a b) -> p a b", b=T)), writes=[bsi[ki]])
            if n % 2 == 0:
                CP("dve", stage_out[ko], stage_in[ki], [bsi[ki]], [bso[ko]])
            else:
                ACT(stage_out[ko], stage_in[ki], AF.Copy, [bsi[ki]], [bso[ko]])
            S.dma("act", sem_so[ko], lambda e: e.dma_start(
                out=sd[i].rearrange("p (a b) -> p a b", b=T), in_=stage_out[ko]), reads=[bso[ko]], writes=[pcs[i]])

        def cast_gen():
            for n, (wd, sd, pcs, i) in enumerate(plist):
                if n < NP1:
                    continue
                cast_piece(n, wd, sd, pcs, i, True)
                yield

        for n, (wd, sd, pcs, i) in enumerate(plist[:NP1]):
            cast_piece(n, wd, sd, pcs, i, False)
        for n, (wd, sd, pcs, i) in enumerate([]):
            ki, ko = n % 4, n % 6
            S.dma("sp", sem_si[ki], lambda e, wd=wd, i=i, ki=ki: e.dma_start(
                out=stage_in[ki], in_=wd[i].rearrange("p (a b) -> p a b", b=T)), writes=[bsi[ki]])
            if n % 2 == 0:
                CP("dve", stage_out[ko], stage_in[ki], [bsi[ki]], [bso[ko]])
            else:
                ACT(stage_out[ko], stage_in[ki], AF.Copy, [bsi[ki]], [bso[ko]])
            S.dma("act", sem_so[ko], lambda e, sd=sd, i=i, ko=ko: e.dma_start(
                out=sd[i].rearrange("p (a b) -> p a b", b=T), in_=stage_out[ko]), reads=[bso[ko]], writes=[pcs[i]])
        for k in range(4):
            pass
        cst = sb("cst", [128, 4], F32)
        MS("dve", cst[:, 0:1], 0.5, [bf("cst")])
        MS("dve", cst[:, 1:2], -0.5, [bf("cst")])
        MS("dve", cst[:, 2:3], 0.25, [bf("cst")])
        MS("dve", cst[:, 3:4], EPS, [bf("cst")])
        MS("dve", ones64[:], 1.0, [bf("ones64")])
        MS("dve", onesm[:], 1.0 / 1024.0, [bf("onesm")])
        MS("dve", xr[:], 0.0, [bf("xr%d" % c) for c in range(4)])
        MS("dve", halo[:], 0.0, [bf("halo")])
        MS("dve", hst[:], 0.0, [bf("hst")])
        MS("dve", kT[:], 0.0, [bf("kT")])
        MS("dve", Vt[:], 0.0, [bf("Vt")])
        MS("dve", graw[:], 0.0, [bf("graw0"), bf("graw1")])
        bpv, bdv = bf("pv"), bf("dv")
        TS("dve", dv[:, 0:4], pv[:, 20:24], 0.5, None, ALU.mult, ..., [bpv], [bdv])
        TS("dve", dv[:, 4:8], pv[:, 24:28], 0.5, None, ALU.mult, ..., [bpv], [bdv])
        ACT(dv[:, 44:48], pv[:, 28:32], AF.Exp, [bpv], [bdv], scale=-1.0)
        TS("dve", dv[:, 40:44], dv[:, 44:48], -0.25, 1.0 / 3.0, ALU.mult, ALU.add, [bdv], [bdv])
        TT("dve", dv[:, 40:44], dv[:, 40:44], dv[:, 44:48], ALU.mult, [bdv], [bdv])
        TS("dve", dv[:, 40:44], dv[:, 40:44], -0.5, None, ALU.add, ..., [bdv], [bdv])
        TT("dve", dv[:, 40:44], dv[:, 40:44], dv[:, 44:48], ALU.mult, [bdv], [bdv])
        TS("dve", dv[:, 40:44], dv[:, 40:44], 1.0, None, ALU.add, ..., [bdv], [bdv])
        TT("dve", dv[:, 40:44], dv[:, 40:44], dv[:, 44:48], ALU.mult, [bdv], [bdv])
        TS("dve", dv[:, 8:12], dv[:, 40:44], -8.0, None, ALU.mult, ..., [bdv], [bdv])
        TS("dve", dv[:, 12:16], dv[:, 40:44], -4.0, None, ALU.mult, ..., [bdv], [bdv])
        ACT(dv[:, 16:20], pv[:, 168:172], AF.Exp, [bpv], [bdv])
        TS("dve", dv[:, 20:28], pv[:, 128:136], 0.5, None, ALU.mult, ..., [bpv], [bdv])
        TS("dve", dv[:, 28:36], pv[:, 136:144], ALPHA, None, ALU.mult, ..., [bpv], [bdv])
        TS("dve", dv[:, 36:40], pv[:, 144:148], ALPHA, None, ALU.mult, ..., [bpv], [bdv])
        ab1 = sb("ab1", [128, 8], F32)
        TS("dve", ab1[:, 0:8], pv[:, 144:152], ALPHA, None, ALU.mult, ..., [bpv], [bf("ab1")])

        class Stream:
            def __init__(self, ring, nslots, sems, scr, pcs, npieces, eng, name):
                self.ring, self.nslots, self.sems, self.scr, self.pcs = ring, nslots, sems, scr, pcs
                self.npieces, self.eng, self.name = npieces, eng, name
                self.total = npieces * ntiles
                self.loaded = 0
                self.pos = 0
                self.slotbuf = [bf("%s_slot%d" % (name, i)) for i in range(nslots)]

            def _load(self, seq):
                slot = seq % self.nslots
                piece = seq % self.npieces
                dst = self.ring[:, slot, :]
                src = self.scr[piece]
                S.dma(self.eng, self.sems[slot], lambda e: e.dma_start(out=dst, in_=src),
                      reads=[self.pcs[piece]], writes=[self.slotbuf[slot]])

            def prefetch(self, upto):
                while self.loaded < min(upto, self.total):
                    self._load(self.loaded)
                    self.loaded += 1

            def blk(self):
                seq, b = divmod(self.pos, 16)
                self.prefetch(seq + self.nslots)
                self.pos += 1
                slot = seq % self.nslots
                return self.ring[:, slot, b * 128:(b + 1) * 128], self.slotbuf[slot]

        st1 = Stream(ring1, R1, sem_r1, s1_d, pc1, NP1, "sp", "r1")
        st2 = Stream(ring2, R2, sem_r2, s2_d, pc2, NP2, "sp", "r2")

        xrot = [0]
        rot3 = [0]
        rot = {"rnn": 0, "exp": 0, "z": 0, "g": 0, "sg": 0, "ot": 0}

        def nxt(k):
            v = rot[k]
            rot[k] = (v + 1) % 2
            return v

        out_evs = []

        xdone = set()

        def xload(s, blk):
            if (s, blk) in xdone or s >= ntiles:
                return
            xdone.add((s, blk))
            r = blk % 2
            src = x_d[s * T + blk * 128: s * T + (blk + 1) * 128, :]
            S.dma("sp", sem_x[r], lambda e: e.dma_start(out=xraw[:, r, :], in_=src), writes=[bf("xraw%d" % r)])

        pdone = set()

        def pload(s):
            if s in pdone or s >= ntiles:
                return
            pdone.add(s)
            src = p_d[s * T:(s + 1) * T, :].rearrange("(n p) f -> p n f", p=128)
            S.dma("sp", sem_p, lambda e: e.dma_start(out=praw[:, :, :], in_=src), writes=[bf("praw")])

        def MA_a(s):
            t0 = s * T
            xb = s % 2
            bxT = [bf("xT%d_%d" % (xb, c)) for c in range(8)]
            bhT = bf("hT")
            if s < 2:
                for c in range(8):
                    kk_ = xb * 2 + c // 4
                    bxT[c].r.extend(bsi[kk_].r)
                    if bsi[kk_].w is not None:
                        bxT[c].r.append(bsi[kk_].w)
            xload(s, 0)
            xload(s, 1)
            for blk in range(4):
                r = blk % 2
                bx = bf("xraw%d" % r)
                for half in range(2):
                    bk = balloc()
                    for k4 in range(4):
                        kc = half * 4 + k4
                        TR(psb[bk][:, k4 * 128:(k4 + 1) * 128], xraw[:, r, kc * 128:(kc + 1) * 128],
                           [bx], [bank[bk]], inc=(k4 == 3))
                    src_ps = psb[bk][:, :].rearrange("p (a b) -> p a b", b=128)
                    ACT(xT[:, xb, half * 4:half * 4 + 4, blk * 128:(blk + 1) * 128], src_ps, AF.Copy,
                        [bank[bk]], bxT[half * 4:half * 4 + 4])
                    CP("dve", hT[:, half * 4:half * 4 + 4, blk * 128:(blk + 1) * 128],
                       xT[:, xb, half * 4:half * 4 + 4, blk * 128:(blk + 1) * 128],
                       bxT[half * 4:half * 4 + 4], [bhT])
                    bfree(bk)
                if blk + 2 < 4:
                    xload(s, blk + 2)
                yield
            for m in range(14):
                bk = balloc()
                for kc in range(8):
                    w, wb = st1.blk()
                    MM(psb[bk][:, :], w, hT[:, kc, :], kc == 0, kc == 7, [wb, bhT], [bank[bk]], inc=(kc == 7))
                if m < 4:
                    ACT(qT[:, m, :], psb[bk][:, :], AF.Copy, [bank[bk]], [bf("qT")], scale=0.125)
                elif m < 6:
                    CP("dve", kT[:, m - 4, 128:640], psb[bk][:, :], [bank[bk]], [bf("kT")])
                elif m < 10:
                    c = m - 6
                    ACT(xr[:, c, 3:515], psb[bk][:, :], AF.Copy, [bank[bk]], [bf("xr%d" % c)])
                else:
                    c = m - 10
                    ACT(gg[:, c, :], psb[bk][:, :], AF.Gelu_apprx_tanh, [bank[bk]], [bf("gg%d" % c)])
                bfree(bk)
                yield
            bk = balloc()
            vblocks = [st1.blk() for _ in range(8)]
            for _ in range(8):
                st1.blk()
            for blk in range(4):
                for kc in range(8):
                    w, wb = vblocks[kc]
                    MM(psb[bk][:, blk * 128:(blk + 1) * 128], hT[:, kc, blk * 128:(blk + 1) * 128], w,
                       kc == 0, kc == 7, [wb, bhT], [bank[bk]], inc=(kc == 7))
            CP("dve", Vt[:, 128:640], psb[bk][:, :], [bank[bk]], [bf("Vt")])
            bfree(bk)
            yield

        def attn_a(s, Q, g):
            Qg = 4 * s + Q
            kbs = [1] if Qg == 0 else [0, 1]
            bq, bk_ = bf("qT"), bf("kT")
            sc = [balloc(), balloc()]
            for hf in range(2):
                n_mm = len(kbs) * 2
                i_mm = 0
                for kb in kbs:
                    for h2 in range(2):
                        hh = h2 * 2 + hf
                        chunk = 2 * g + hh // 2
                        i_mm += 1
                        MM(psb[sc[hf]][:, (kb * 2 + h2) * 128:(kb * 2 + h2 + 1) * 128],
                           kT[hf * 64:(hf + 1) * 64, g, (Q + kb) * 128:(Q + kb + 1) * 128],
                           qT[hf * 64:(hf + 1) * 64, chunk, Q * 128:(Q + 1) * 128],
                           True, True, [bq, bk_], [bank[sc[hf]]], inc=(i_mm == n_mm))
            r = nxt("exp")
            bex = bf("expT%d" % r)
            c0 = 256 if Qg == 0 else 0
            for hf in range(2):
                ACT(expT[:, r, hf * 512 + c0:(hf + 1) * 512], psb[sc[hf]][:, c0:512], AF.Exp, [bank[sc[hf]]], [bex])
                bfree(sc[hf])
            for hf in range(2):
                TT("pool", expT[:, r, hf * 512 + c0:(hf + 1) * 512], expT[:, r, hf * 512 + c0:(hf + 1) * 512],
                   maskt[:, hf * 512 + c0:(hf + 1) * 512], ALU.mult, [bex, bf("mask")], [bex])
            return (r, kbs)

        def attn_b(s, Q, g, state):
            r, kbs = state
            bv, bmix, bex = bf("Vt"), bf("mixT"), bf("expT%d" % r)
            pvb = balloc()
            for hh in range(4):
                cc = hh // 2
                hf = hh % 2
                for which in range(2):
                    col = which * 256 + cc * 128
                    for ki, kb in enumerate(kbs):
                        lhsT = (Vt[:, (Q + kb) * 128 + g * 64:(Q + kb) * 128 + g * 64 + 64]
                                if which == 0 else ones64[:, :])
                        last = (hh == 3 and which == 1 and ki == len(kbs) - 1)
                        ecol = hf * 512 + (kb * 2 + cc) * 128
                        MM(psb[pvb][hf * 64:(hf + 1) * 64, col:col + 128], lhsT,
                           expT[:, r, ecol:ecol + 128],
                           ki == 0, ki == len(kbs) - 1,
                           [bv, bex, bf("ones64")], [bank[pvb]], inc=last)
            rd = nxt("rnn")
            brd = bf("rden%d" % rd)
            for cc in range(2):
                chunk = 2 * g + cc
                TS("dve", rden[:, rd, cc * 128:(cc + 1) * 128], psb[pvb][:, 256 + cc * 128:256 + (cc + 1) * 128],
                   dv[:, 16 + chunk:17 + chunk], None, ALU.add, ..., [bank[pvb], bdv], [brd])
                S.op("dve", lambda e, cc=cc: e.reciprocal(rden[:, rd, cc * 128:(cc + 1) * 128],
                                                          rden[:, rd, cc * 128:(cc + 1) * 128]), [brd], [brd])
                TT("dve", mixT[:, chunk, Q * 128:(Q + 1) * 128], psb[pvb][:, cc * 128:(cc + 1) * 128],
                   rden[:, rd, cc * 128:(cc + 1) * 128], ALU.mult, [bank[pvb], brd], [bmix])
            bfree(pvb)

        def rnn_a(s, c):
            r = nxt("g")
            bxr, bxc, bxcb = bf("xr%d" % c), bf("xc%d" % r), bf("xcb%d" % r)
            TS("dve", xc[:, r, :], xr[:, c, 0:512], pv[:, 0 * 4 + c:0 * 4 + c + 1], pv[:, 16 + c:17 + c],
               ALU.mult, ALU.add, [bxr, bpv], [bxc])
            for k in range(1, 4):
                STT("dve", xc[:, r, :], xr[:, c, k:k + 512], pv[:, k * 4 + c:k * 4 + c + 1], xc[:, r, :],
                    ALU.mult, ALU.add, [bxr, bpv, bxc], [bxc])
            CP("dve", xr[:, c, 0:3], xr[:, c, 512:515], [bxr], [bxr])
            ACT(xcb[:, r, :], xc[:, r, :], AF.Copy, [bxc], [bxcb])
            return r

        def rnn_b(s, c, r):
            bxcb = bf("xcb%d" % r)
            brr, bii, baa = bf("rr%d" % r), bf("ii%d" % r), bf("aa%d" % r)
            ba_, bx_ = balloc(), balloc()
            MM(psb[ba_][:, :], gw[:, (c * 2) * 128:(c * 2 + 1) * 128], xcb[:, r, :], True, True,
               [bf("gw"), bxcb], [bank[ba_]], inc=True)
            MM(psb[bx_][:, :], gw[:, (c * 2 + 1) * 128:(c * 2 + 2) * 128], xcb[:, r, :], True, True,
               [bf("gw"), bxcb], [bank[bx_]], inc=True)
            ACT(rr[:, r, :], psb[ba_][:, :], AF.Tanh, [bank[ba_], bdv], [brr], bias=dv[:, c:c + 1], scale=0.5)
            ACT(ii[:, r, :], psb[bx_][:, :], AF.Tanh, [bank[bx_], bdv], [bii], bias=dv[:, 4 + c:5 + c], scale=0.5)
            bfree(ba_)
            bfree(bx_)
            ACT(aa[:, r, :], rr[:, r, :], AF.Exp, [brr, bdv], [baa], bias=dv[:, 12 + c:13 + c],
                scale=dv[:, 12 + c:13 + c])
            ACT(rr[:, r, :], rr[:, r, :], AF.Exp, [brr, bdv], [brr], bias=dv[:, 8 + c:9 + c],
                scale=dv[:, 8 + c:9 + c])
            ACT(rr[:, r, :], rr[:, r, :], AF.Sqrt, [brr, bf("cst")], [brr], bias=cst[:, 2:3], scale=-0.25)

        def rnn_c(s, c, r):
            bxc = bf("xc%d" % r)
            brr, bii, baa, bhs = bf("rr%d" % r), bf("ii%d" % r), bf("aa%d" % r), bf("hs%d" % r)
            bhst, bmix = bf("hst"), bf("mixT")
            STT("dve", ii[:, r, :], ii[:, r, :], 1.0, xc[:, r, :], ALU.add, ALU.mult, [bii, bxc], [bii])
            TT("dve", rr[:, r, :], rr[:, r, :], ii[:, r, :], ALU.mult, [brr, bii], [brr])
            S.op("dve", lambda e: e.tensor_tensor_scan(hs[:, r, :], aa[:, r, :], rr[:, r, :], hst[:, c:c + 1],
                                                       ALU.mult, ALU.add),
                 [baa, brr, bhst], [bhs])
            CP("dve", hst[:, c:c + 1], hs[:, r, 511:512], [bhs], [bhst])
            TT("dve", mixT[:, 4 + c, :], hs[:, r, :], gg[:, c, :], ALU.mult, [bhs, bf("gg%d" % c)], [bmix])

        def MA_b(s):
            xload(s + 1, 0)
            xload(s + 1, 1)
            for Q in range(4):
                r = rnn_a(s, Q)
                yield
                sa0 = attn_a(s, Q, 0)
                yield
                sa1 = attn_a(s, Q, 1)
                yield
                rnn_b(s, Q, r)
                yield
                attn_b(s, Q, 0, sa0)
                yield
                attn_b(s, Q, 1, sa1)
                yield
                rnn_c(s, Q, r)
                yield
            CP("dve", kT[:, :, 0:128], kT[:, :, 512:640], [bf("kT")], [bf("kT")])
            CP("dve", Vt[:, 0:128], Vt[:, 512:640], [bf("Vt")], [bf("Vt")])
            yield

        def ln_stats_prep(zsrc, bz):
            r = nxt("z")
            bzb, bzq = bf("zb%d" % r), bf("zq%d" % r)
            ACT(zb[:, r, :], zsrc, AF.Copy, [bz], [bzb])
            ACT(zq[:, r, :], zsrc, AF.Square, [bz], [bzq])
            return r

        def ln_stats_mm(st, m, r):
            bzb, bzq = bf("zb%d" % r), bf("zq%d" % r)
            MM(psb[st[0]][:, :], onesm[:, :], zb[:, r, :], m == 0, m == 7, [bf("onesm"), bzb], [bank[st[0]]],
               inc=(m == 7))
            MM(psb[st[1]][:, :], onesm[:, :], zq[:, r, :], m == 0, m == 7, [bf("onesm"), bzq], [bank[st[1]]],
               inc=(m == 7))

        def ln_finish(st):
            bl = bf("lnm")
            ACT(lnm[:, 2, :], psb[st[0]][:, :], AF.Copy, [bank[st[0]]], [bl])
            TT("dve", lnm[:, 0, :], lnm[:, 2, :], lnm[:, 2, :], ALU.mult, [bl], [bl])
            STT("dve", lnm[:, 0, :], lnm[:, 0, :], -1.0, psb[st[1]][:, :], ALU.mult, ALU.add,
                [bl, bank[st[1]]], [bl])
            ACT(lnm[:, 0, :], lnm[:, 0, :], AF.Sqrt, [bl, bf("cst")], [bl], bias=cst[:, 3:4], scale=1.0)
            S.op("dve", lambda e: e.reciprocal(lnm[:, 0, :], lnm[:, 0, :]), [bl], [bl])
            STT("dve", lnm[:, 1, :], lnm[:, 2, :], -1.0, lnm[:, 0, :], ALU.mult, ALU.mult, [bl], [bl])
            bfree(st[0])
            bfree(st[1])

        def MC(s):
            xb = s % 2
            bxT = [bf("xT%d_%d" % (xb, c)) for c in range(8)]
            bmix, bl, bh1 = bf("mixT"), bf("lnm"), bf("h1T")
            pload(s)
            st = (balloc(), balloc())
            for m in range(8):
                bk = balloc()
                for kc in range(8):
                    w, wb = st2.blk()
                    MM(psb[bk][:, :], w, mixT[:, kc, :], kc == 0, kc == 7, [wb, bmix], [bank[bk]], inc=(kc == 7))
                STT("dve", xT[:, xb, m, :], xT[:, xb, m, :], ALPHA, psb[bk][:, :], ALU.mult, ALU.add,
                    [bxT[m], bank[bk]], [bxT[m]])
                bfree(bk)
                if m > 0:
                    ln_stats_mm(st, m - 1, rprev)
                rprev = ln_stats_prep(xT[:, xb, m, :], bxT[m])
                yield
            ln_stats_mm(st, 7, rprev)
            ln_finish(st)
            yield
            for m in range(8):
                TT("dve", xT[:, xb, m, :], xT[:, xb, m, :], lnm[:, 0, :], ALU.mult, [bxT[m], bl], [bxT[m]])
                TT("pool", xT[:, xb, m, :], xT[:, xb, m, :], lnm[:, 1, :], ALU.add, [bxT[m], bl], [bxT[m]])
                ACT(h1T[:, m, :], xT[:, xb, m, :], AF.Identity, [bxT[m], bpv], [bh1],
                    bias=pv[:, 144 + m:145 + m], scale=pv[:, 136 + m:137 + m])
                ACT(xT[:, xb, m, :], xT[:, xb, m, :], AF.Identity, [bxT[m], bdv, bf("ab1")], [bxT[m]],
                    bias=ab1[:, m:m + 1], scale=dv[:, 28 + m:29 + m])
                yield

        def F_up(s):
            bh1, bact, bhalo = bf("h1T"), bf("actb"), bf("halo")
            if s == 0:
                for k in range(6):
                    bact.r.extend(bso[k].r)
                    if bso[k].w is not None:
                        bact.r.append(bso[k].w)

            def stage_b(j, ry, bv_):
                byy = bf("yy%d" % ry)
                ACT(yy[:, ry, :], yy[:, ry, :], AF.Gelu_apprx_tanh, [byy], [byy])
                TT("dve", actb[:, j, :], yy[:, ry, :], psb[bv_][:, :], ALU.mult, [byy, bank[bv_]], [bact])
                bfree(bv_)

            pend = None
            for j in range(24):
                bg_, bv_ = balloc(), balloc()
                for kc in range(8):
                    w, wb = st2.blk()
                    MM(psb[bg_][:, :], w, h1T[:, kc, :], kc == 0, kc == 7, [wb, bh1], [bank[bg_]], inc=(kc == 7))
                for kc in range(8):
                    w, wb = st2.blk()
                    MM(psb[bv_][:, :], w, h1T[:, kc, :], kc == 0, kc == 7, [wb, bh1], [bank[bv_]], inc=(kc == 7))
                r = nxt("g")
                ry = rot3[0]
                rot3[0] = (ry + 1) % 3
                bgr, byy = bf("graw%d" % r), bf("yy%d" % ry)
                CP("dve", graw[:, r, 0:2], halo[:, j, :], [bhalo], [bgr])
                ACT(graw[:, r, 2:514], psb[bg_][:, :], AF.Copy, [bank[bg_]], [bgr])
                ACT(halo[:, j, :], psb[bg_][:, 510:512], AF.Copy, [bank[bg_]], [bhalo])
                ACT(yy[:, ry, :], psb[bg_][:, :], AF.Identity, [bank[bg_], bpv], [byy],
                    bias=pv[:, 104 + j:105 + j], scale=pv[:, 32 + 2 * 24 + j:33 + 2 * 24 + j])
                bfree(bg_)
                for k in range(0, 2):
                    STT("dve", yy[:, ry, :], graw[:, r, k:k + 512], pv[:, 32 + k * 24 + j:33 + k * 24 + j],
                        yy[:, ry, :], ALU.mult, ALU.add, [bgr, bpv, byy], [byy])
                if pend is not None:
                    stage_b(*pend)
                pend = (j, ry, bv_)
                yield
            stage_b(*pend)
            yield

        def F_down(s):
            t0 = s * T
            xb = s % 2
            bxT = [bf("xT%d_%d" % (xb, c)) for c in range(8)]
            bh1, bact, bl, bpr, bpT = bf("h1T"), bf("actb"), bf("lnm"), bf("praw"), bf("pT")
            pload(s)
            for k2 in range(2):
                bk = balloc()
                for blk in range(4):
                    TR(psb[bk][:, blk * 128:(blk + 1) * 128], praw[:, blk, k2 * 128:(k2 + 1) * 128],
                       [bpr], [bank[bk]], inc=(blk == 3))
                CP("dve", pT[:, k2, :], psb[bk][:, :], [bank[bk]], [bpT])
                bfree(bk)
            yield
            for m in range(8):
                bg_, bp_ = balloc(), balloc()
                for kc in range(8):
                    w, wb = st2.blk()
                    MM(psb[bg_][:, :], w, h1T[:, kc, :], kc == 0, kc == 7, [wb, bh1], [bank[bg_]], inc=(kc == 7))
                for k2 in range(2):
                    w, wb = st2.blk()
                    MM(psb[bp_][:, :], w, pT[:, k2, :], k2 == 0, k2 == 1, [wb, bpT], [bank[bp_]], inc=(k2 == 1))
                r = nxt("sg")
                bsg = bf("sg%d" % r)
                ACT(sg[:, r, :], psb[bg_][:, :], AF.Tanh, [bank[bg_], bdv], [bsg], bias=dv[:, 20 + m:21 + m],
                    scale=0.5)
                bfree(bg_)
                STT("dve", sg[:, r, :], sg[:, r, :], 1.0, psb[bp_][:, :], ALU.add, ALU.mult,
                    [bsg, bank[bp_]], [bsg])
                bfree(bp_)
                STT("dve", xT[:, xb, m, :], sg[:, r, :], 0.5, xT[:, xb, m, :], ALU.mult, ALU.add,
                    [bsg, bxT[m]], [bxT[m]])
                yield
            st = (balloc(), balloc())
            for m in range(8):
                bk = balloc()
                for c in range(24):
                    w, wb = st2.blk()
                    MM(psb[bk][:, :], w, actb[:, c, :], c == 0, c == 23, [wb, bact], [bank[bk]], inc=(c == 23))
                TT("dve", xT[:, xb, m, :], xT[:, xb, m, :], psb[bk][:, :], ALU.add, [bxT[m], bank[bk]], [bxT[m]])
                bfree(bk)
                if m > 0:
                    ln_stats_mm(st, m - 1, rprev)
                rprev = ln_stats_prep(xT[:, xb, m, :], bxT[m])
                yield
            ln_stats_mm(st, 7, rprev)
            ln_finish(st)
            yield
            for m in range(8):
                TT("dve", xT[:, xb, m, :], xT[:, xb, m, :], lnm[:, 0, :], ALU.mult, [bxT[m], bl], [bxT[m]])
                TT("pool", xT[:, xb, m, :], xT[:, xb, m, :], lnm[:, 1, :], ALU.add, [bxT[m], bl], [bxT[m]])
                ACT(xT[:, xb, m, :], xT[:, xb, m, :], AF.Identity, [bxT[m], bpv], [bxT[m]],
                    bias=pv[:, 160 + m:161 + m], scale=pv[:, 152 + m:153 + m])
                yield
            for blk in range(4):
                r = 0
                ro = nxt("ot")
                bot = bf("ot%d" % r)
                for half in range(2):
                    bk = balloc()
                    for k4 in range(4):
                        m = half * 4 + k4
                        TR(psb[bk][:, k4 * 128:(k4 + 1) * 128], xT[:, xb, m, blk * 128:(blk + 1) * 128],
                           [bxT[m]], [bank[bk]], inc=(k4 == 3))
                    if half == 0:
                        ACT(ot[:, r, 0:512], psb[bk][:, :], AF.Copy, [bank[bk]], [bot])
                    else:
                        CP("dve", ot[:, r, 512:1024], psb[bk][:, :], [bank[bk]], [bot])
                    bfree(bk)
                dst = out_d[t0 + blk * 128:t0 + (blk + 1) * 128, :]
                ev = S.dma("sp", sem_o[ro], lambda e, r=r, dst=dst: e.dma_start(out=dst, in_=ot[:, r, :]),
                           reads=[bot], writes=[bf("outd%d" % ro)])
                out_evs.append(ev)
                yield

        if dbg == 0:
            cg = cast_gen()
            for s in range(ntiles + 1):
                g1 = MA_a(s) if s < ntiles else None
                g2 = chain(MC(s - 1), F_up(s - 1)) if s >= 1 else None
                if s == 0:
                    interleave(g1, take(cg, 22))
                else:
                    interleave_pat(g1, g2, "ABABABABBBBBB" + "AABAABAAB" + "ABABABABAB")
                g3 = MA_b(s) if s < ntiles else None
                g4 = F_down(s - 1) if s >= 1 else None
                if s == 0:
                    interleave(g3, cg)
                else:
                    interleave_pat(g3, g4, "AAB" * 15)
            assert st1.pos == 128 * ntiles and st2.pos == 720 * ntiles, (st1.pos, st2.pos)
        else:
            interleave(cast_gen())
            stages = [MA_a, MA_b, MC, F_up, F_down]
            for st_ in stages[:dbg - 1]:
                interleave(st_(0))
        S.wait_events("sp", [e_ for e_ in out_evs[-2:] if e_ is not None] + [S.last[k] for k in sem_so if k in S.last])
        if dbg or limit:
            S.drain("sp")
        S.emit()
    return nc


def _blocks_to_pieces(blocks):
    n = len(blocks)
    assert n % 16 == 0
    arr = np.stack(blocks, 0).reshape(n // 16, 16, 128, 128)
    return np.ascontiguousarray(arr.transpose(0, 2, 1, 3).reshape(n // 16, 128, 2048))


def prep_weights(w_in, w_out, w_ffn_up, w_ffn_down, ple_gate_w, ple_proj):
    w_in, w_out, w_up, w_dn, wg, wp = (np.asarray(a[0], np.float32) for a in
                                       (w_in, w_out, w_ffn_up, w_ffn_down, ple_gate_w, ple_proj))
    cols = [w_in[:, 0:512],
            w_in[:, 512:576], w_in[:, 512:576],
            w_in[:, 576:640], w_in[:, 576:640],
            w_in[:, 768:1280], w_in[:, 1280:1792], w_in[:, 640:768],
            np.zeros((1024, 128), np.float32)]
    wi = np.concatenate(cols, axis=1)
    assert wi.shape == (1024, 2048)
    b1 = [wi[kc * 128:(kc + 1) * 128, m * 128:(m + 1) * 128] for m in range(16) for kc in range(8)]
    w1 = _blocks_to_pieces(b1)
    b2 = []
    for m in range(8):
        for kc in range(8):
            b2.append(w_out[kc * 128:(kc + 1) * 128, m * 128:(m + 1) * 128])
    for j in range(24):
        for kc in range(8):
            b2.append(w_up[kc * 128:(kc + 1) * 128, j * 128:(j + 1) * 128])
        for kc in range(8):
            b2.append(w_up[kc * 128:(kc + 1) * 128, 3072 + j * 128:3072 + (j + 1) * 128])
    for m in range(8):
        for kc in range(8):
            b2.append(wg[kc * 128:(kc + 1) * 128, m * 128:(m + 1) * 128])
        for k2 in range(2):
            b2.append(wp[k2 * 128:(k2 + 1) * 128, m * 128:(m + 1) * 128])
    for m in range(8):
        for c in range(24):
            b2.append(w_dn[c * 128:(c + 1) * 128, m * 128:(m + 1) * 128])
    w2 = _blocks_to_pieces(b2)
    assert w1.shape[0] == NP1 and w2.shape[0] == NP2
    return w1, w2


def prep_small(attn_sinks, rnn_conv_w, rnn_conv_b, gate_a_w, gate_a_b, gate_x_w, gate_x_b, lru_lambda,
               ln1_g, ln1_b, ffn_conv_w, ffn_conv_b, ple_gate_b, ln2_g, ln2_b):
    f = lambda a: np.asarray(a[0], np.float32)
    pv = np.zeros((128, NPV), np.float32)
    cw = f(rnn_conv_w)
    for k in range(4):
        pv[:, k * 4:(k + 1) * 4] = cw[k].reshape(4, 128).T
    pv[:, 16:20] = f(rnn_conv_b).reshape(4, 128).T
    pv[:, 20:24] = f(gate_a_b).reshape(4, 128).T
    pv[:, 24:28] = f(gate_x_b).reshape(4, 128).T
    pv[:, 28:32] = f(lru_lambda).reshape(4, 128).T
    fw = f(ffn_conv_w)
    for k in range(3):
        pv[:, 32 + k * 24:32 + (k + 1) * 24] = fw[k].reshape(24, 128).T
    pv[:, 104:128] = f(ffn_conv_b).reshape(24, 128).T
    pv[:, 128:136] = f(ple_gate_b).reshape(8, 128).T
    pv[:, 136:144] = f(ln1_g).reshape(8, 128).T
    pv[:, 144:152] = f(ln1_b).reshape(8, 128).T
    pv[:, 152:160] = f(ln2_g).reshape(8, 128).T
    pv[:, 160:168] = f(ln2_b).reshape(8, 128).T
    sk = f(attn_sinks)
    for m in range(4):
        pv[0:64, 168 + m] = sk[2 * m]
        pv[64:128, 168 + m] = sk[2 * m + 1]
    gw = np.zeros((128, 1024), np.float32)
    for c in range(4):
        for gi, wsrc in enumerate((f(gate_a_w), f(gate_x_w))):
            blk = np.zeros((128, 128), np.float32)
            blk[0:64, 0:64] = wsrc[2 * c]
            blk[64:128, 64:128] = wsrc[2 * c + 1]
            gw[:, (c * 2 + gi) * 128:(c * 2 + gi + 1) * 128] = blk
    s_idx = np.arange(128)[:, None]
    q_idx = np.arange(128)[None, :]
    mprev = (s_idx > q_idx).astype(np.float32)
    mcur = (s_idx <= q_idx).astype(np.float32)
    mask = np.concatenate([mprev, mprev, mcur, mcur, mprev, mprev, mcur, mcur], axis=1)
    ident = np.eye(128, dtype=np.float32)
    return pv, gw, np.ascontiguousarray(mask), ident


_NC_CACHE = {}


def run(inputs, NT, trace=False):
    x = np.asarray(inputs["x"], np.float32)
    p = np.asarray(inputs["p"], np.float32)[0]
    w1, w2 = prep_weights(inputs["w_in"], inputs["w_out"], inputs["w_ffn_up"], inputs["w_ffn_down"],
                          inputs["ple_gate_w"], inputs["ple_proj"])
    pv, gw, mask, ident = prep_small(inputs["attn_sinks"], inputs["rnn_conv_w"], inputs["rnn_conv_b"],
                                     inputs["gate_a_w"], inputs["gate_a_b"], inputs["gate_x_w"],
                                     inputs["gate_x_b"], inputs["lru_lambda"], inputs["ln1_g"], inputs["ln1_b"],
                                     inputs["ffn_conv_w"], inputs["ffn_conv_b"], inputs["ple_gate_b"],
                                     inputs["ln2_g"], inputs["ln2_b"])
    if NT not in _NC_CACHE:
        _NC_CACHE[NT] = build(NT)
    nc = _NC_CACHE[NT]
    in_maps = []
    for b in range(NB):
        in_maps.append({"x": np.ascontiguousarray(x[b, :NT]), "p": np.ascontiguousarray(p[b, :NT]),
                        "w1": w1, "w2": w2, "gw": gw, "mask": mask, "ident": ident, "pv": pv})
    res = run_bass_kernel_spmd(nc, in_maps, core_ids=list(range(NB)), trace=trace)
    out = np.stack([r["out"] for r in res.results], 0)
    return out, res


def kernel(**inputs):
    out, _ = run(inputs, SEQ)
    return out.astype(np.float32)
```

```python
from contextlib import ExitStack
import numpy as np
import concourse.bass as bass
import concourse.mybir as mybir
from concourse.bass_utils import run_bass_kernel_spmd

F32 = mybir.dt.float32
BF16 = mybir.dt.bfloat16
AF = mybir.ActivationFunctionType
ALU = mybir.AluOpType

D = 1024
SEQ = 8192
NB = 8
T = 512
ALPHA = float(2.0 ** 0.25)
EPS = 1e-5
R1 = 2
R2 = 4
NP1 = 8
NP2 = 45
NPV = 172
ENGS = ("pe", "act", "dve", "pool", "sp")


class Buf:
    __slots__ = ("name", "w", "r", "excl")

    def __init__(self, name, excl=False):
        self.name = name
        self.w = None
        self.r = []
        self.excl = excl


class Ev:
    __slots__ = ("key", "val", "eng", "snap")

    def __init__(self, key, val, eng, snap):
        self.key = key
        self.val = val
        self.eng = eng
        self.snap = snap


class Sched:
    def __init__(self, nc, ctx):
        self.nc = nc
        self.ctx = ctx
        self.ops = {e: [] for e in ENGS}
        self.cnt = {}
        self.seen = {e: {} for e in ENGS}
        self.sems = {}
        self.last = {}
        for e in ENGS:
            if e != "sp":
                self.sems[e] = ctx.enter_context(nc.semaphore("s_" + e))
                self.cnt[e] = 0
        self.n_dma_sem = 0
        self.pending = {e: False for e in ENGS}
        self.total = 0
        self.limit = 1 << 60

    def dma_sem(self):
        self.n_dma_sem += 1
        key = "dma%d" % self.n_dma_sem
        self.sems[key] = self.ctx.enter_context(self.nc.semaphore(key))
        self.cnt[key] = 0
        return key

    def _need(self, eng, reads, writes, extra=()):
        need = {}

        def req(ev, kind):
            if ev is None:
                return
            if ev.key == eng:
                if eng == "pe":
                    return
                if kind == "war":
                    return
            if need.get(ev.key) is None or need[ev.key].val < ev.val:
                need[ev.key] = ev

        for b in reads:
            req(b.w, "raw")
            if b.excl:
                for r in b.r:
                    if r.key != eng:
                        req(r, "raw")
        for b in writes:
            req(b.w, "waw")
            for r in b.r:
                req(r, "war")
        for ev in extra:
            req(ev, "raw")
        seen = self.seen[eng]
        waits = []
        for k, ev in need.items():
            if seen.get(k, 0) < ev.val:
                waits.append((k, ev.val))
        for k, ev in need.items():
            if seen.get(k, 0) < ev.val:
                seen[k] = ev.val
            for k2, v2 in ev.snap.items():
                if seen.get(k2, 0) < v2:
                    seen[k2] = v2
        return waits

    def _commit(self, ev, reads, writes):
        for b in reads:
            b.r.append(ev)
            if len(b.r) > 64:
                best = {}
                for r in b.r:
                    if best.get(r.key) is None or best[r.key].val < r.val:
                        best[r.key] = r
                b.r = list(best.values())
        for b in writes:
            b.w = ev
            b.r = []

    def op(self, eng, fn, reads=(), writes=(), inc=True):
        self.total += 1
        if self.total > self.limit:
            return None
        waits = self._need(eng, reads, writes)
        if inc:
            self.cnt[eng] += 1
            val = self.cnt[eng]
            self.pending[eng] = False
        else:
            val = self.cnt[eng] + 1
            self.pending[eng] = True
        ev = Ev(eng, val, eng, dict(self.seen[eng]))
        self._commit(ev, reads, writes)
        self.ops[eng].append((waits, fn, eng if inc else None, 1))
        return ev

    def dma(self, eng, semkey, fn, reads=(), writes=()):
        self.total += 1
        if self.total > self.limit:
            return None
        extra = (self.last[semkey],) if semkey in self.last else ()
        waits = self._need(eng, reads, writes, extra)
        self.cnt[semkey] += 16
        ev = Ev(semkey, self.cnt[semkey], eng, dict(self.seen[eng]))
        self.last[semkey] = ev
        self._commit(ev, reads, writes)
        self.ops[eng].append((waits, fn, semkey, 16))
        return ev

    def wait_events(self, eng, evs):
        waits = self._need(eng, (), (), evs)
        self.ops[eng].append((waits, None, None, 0))

    def drain(self, eng):
        waits = []
        for k, v in self.cnt.items():
            if v > 0 and self.seen[eng].get(k, 0) < v:
                waits.append((k, v))
                self.seen[eng][k] = v
        self.ops[eng].append((waits, None, None, 0))

    def emit(self):
        nc = self.nc
        sems = self.sems
        ops = self.ops

        def replay(engh, name):
            for waits, fn, semkey, inc in ops[name]:
                for k, v in waits:
                    engh.wait_ge(sems[k], v)
                if fn is None:
                    continue
                ins = fn(engh)
                if semkey is not None:
                    ins.then_inc(sems[semkey], inc)

        with nc.Block() as block:
            @block.tensor
            def _(e):
                replay(e, "pe")

            @block.scalar
            def _(e):
                replay(e, "act")

            @block.vector
            def _(e):
                replay(e, "dve")

            @block.gpsimd
            def _(e):
                replay(e, "pool")

            @block.sync
            def _(e):
                replay(e, "sp")


MARKS = []
_SREF = [None]


def interleave(*gens):
    gens = [g for g in gens if g is not None]
    while gens:
        for g in list(gens):
            try:
                next(g)
                MARKS.append(_SREF[0].total)
            except StopIteration:
                gens.remove(g)


def interleave_pat(ga, gb, pat):
    gens = {"A": ga, "B": gb}
    alive = {k for k, g in gens.items() if g is not None}
    i = 0
    while alive:
        if i < len(pat):
            k = pat[i]
        else:
            k = "AB"[(i - len(pat)) % 2]
        i += 1
        if k not in alive:
            k = next(iter(alive))
        try:
            next(gens[k])
            MARKS.append(_SREF[0].total)
        except StopIteration:
            alive.discard(k)


def take(gen, n):
    for _ in range(n):
        try:
            next(gen)
        except StopIteration:
            return
        yield


def chain(*gens):
    for g in gens:
        if g is not None:
            yield from g


def build(NT, dbg=0, limit=None):
    ntiles = NT // T
    nc = bass.Bass("TRN2", target_bir_lowering=False)
    x_d = nc.dram_tensor("x", [NT, D], F32, kind="ExternalInput").ap()
    p_d = nc.dram_tensor("p", [NT, 256], F32, kind="ExternalInput").ap()
    w1_d = nc.dram_tensor("w1", [NP1, 128, 2048], F32, kind="ExternalInput").ap()
    w2_d = nc.dram_tensor("w2", [NP2, 128, 2048], F32, kind="ExternalInput").ap()
    gw_d = nc.dram_tensor("gw", [128, 1024], F32, kind="ExternalInput").ap()
    mask_d = nc.dram_tensor("mask", [128, 1024], F32, kind="ExternalInput").ap()
    ident_d = nc.dram_tensor("ident", [128, 128], F32, kind="ExternalInput").ap()
    pv_d = nc.dram_tensor("pv", [128, NPV], F32, kind="ExternalInput").ap()
    out_d = nc.dram_tensor("out", [NT, D], F32, kind="ExternalOutput").ap()
    s1_d = nc.dram_tensor("s1", [NP1, 128, 2048], BF16).ap()
    s2_d = nc.dram_tensor("s2", [NP2, 128, 2048], BF16).ap()

    with ExitStack() as ctx:
        S = Sched(nc, ctx)
        _SREF[0] = S
        MARKS.append(-1)
        if limit:
            S.limit = limit

        def sb(name, shape, dt):
            return ctx.enter_context(nc.sbuf_tensor("sb_" + name, shape, dt))

        ident = sb("ident", [128, 128], F32)
        maskt = sb("maskt", [128, 1024], BF16)
        ones64 = sb("ones64", [128, 64], BF16)
        onesm = sb("onesm", [128, 128], BF16)
        pv = sb("pv", [128, NPV], F32)
        dv = sb("dv", [128, 48], F32)
        gw = sb("gw", [128, 1024], BF16)
        ring1 = sb("ring1", [128, R1, 2048], BF16)
        ring2 = sb("ring2", [128, R2, 2048], BF16)
        xraw = sb("xraw", [128, 2, 1024], F32)
        xT = sb("xT", [128, 2, 8, T], F32)
        hT = sb("hT", [128, 8, T], BF16)
        qT = sb("qT", [128, 4, T], BF16)
        kT = sb("kT", [128, 2, 640], BF16)
        Vt = sb("Vt", [128, 640], BF16)
        xr = sb("xr", [128, 4, 516], F32)
        gg = sb("gg", [128, 4, T], F32)
        xc = sb("xc", [128, 2, T], F32)
        xcb = sb("xcb", [128, 2, T], BF16)
        rr = sb("rr", [128, 2, T], F32)
        ii = sb("ii", [128, 2, T], F32)
        aa = sb("aa", [128, 2, T], F32)
        hs = sb("hs", [128, 2, T], F32)
        hst = sb("hst", [128, 4], F32)
        expT = sb("expT", [128, 2, 1024], BF16)
        rden = sb("rden", [128, 2, 256], F32)
        mixT = sb("mixT", [128, 8, T], BF16)
        h1T = sb("h1T", [128, 8, T], BF16)
        zb = sb("zb", [128, 2, T], BF16)
        zq = sb("zq", [128, 2, T], BF16)
        lnm = sb("lnm", [128, 3, T], F32)
        graw = sb("graw", [128, 2, 516], F32)
        yy = sb("yy", [128, 3, T], F32)
        halo = sb("halo", [128, 24, 2], F32)
        actb = sb("actb", [128, 24, T], BF16)
        praw = sb("praw", [128, 4, 256], F32)
        pT = sb("pT", [128, 2, T], BF16)
        sg = sb("sg", [128, 2, T], F32)
        ot = sb("ot", [128, 1, 1024], F32)
        psb = [ctx.enter_context(nc.psum_tensor("ps%d" % i, [128, 512], F32)) for i in range(8)]

        B = {}

        def bf(name):
            if name not in B:
                B[name] = Buf(name)
            return B[name]

        bank = [bf("bank%d" % i) for i in range(8)]
        for b_ in bank:
            b_.excl = True
        free_banks = list(range(8))

        def balloc():
            assert free_banks, "out of PSUM banks"
            return free_banks.pop(0)

        def bfree(i):
            free_banks.append(i)

        def ACT(out, in_, func, reads, writes, bias=None, scale=None):
            kw = {}
            if bias is not None:
                kw["bias"] = bias
            if scale is not None:
                kw["scale"] = scale
            S.op("act", lambda e: e.activation(out, in_, func, **kw), reads, writes)

        def TT(eng, out, in0, in1, op, reads, writes):
            S.op(eng, lambda e: e.tensor_tensor(out, in0, in1, op), reads, writes)

        def STT(eng, out, in0, scalar, in1, op0, op1, reads, writes):
            S.op(eng, lambda e: e.scalar_tensor_tensor(out, in0, scalar, in1, op0, op1), reads, writes)

        def TS(eng, out, in0, s1, s2, op0, op1, reads, writes):
            if op1 is Ellipsis:
                S.op(eng, lambda e: e.tensor_scalar(out, in0, s1, s2, op0), reads, writes)
            else:
                S.op(eng, lambda e: e.tensor_scalar(out, in0, s1, s2, op0, op1), reads, writes)

        def CP(eng, out, in_, reads, writes):
            S.op(eng, lambda e: e.tensor_copy(out, in_), reads, writes)

        def MM(out, lhsT, rhs, start, stop, reads, writes, inc):
            S.op("pe", lambda e: e.matmul(out, lhsT, rhs, start=start, stop=stop), reads, writes, inc=inc)

        def TR(out, in_, reads, writes, inc):
            S.op("pe", lambda e: e.transpose(out, in_, ident[:]), reads + [bf("ident")], writes, inc=inc)

        def MS(eng, ap, val, writes):
            S.op(eng, lambda e: e.memset(ap, val), (), writes)

        sem_r1 = [S.dma_sem() for _ in range(R1)]
        sem_r2 = [S.dma_sem() for _ in range(R2)]
        sem_x = [S.dma_sem() for _ in range(2)]
        sem_p = S.dma_sem()
        sem_o = [S.dma_sem() for _ in range(2)]
        sem_c = [S.dma_sem() for _ in range(4)]

        S.dma("sp", sem_c[0], lambda e: e.dma_start(out=ident[:], in_=ident_d), writes=[bf("ident")])
        S.dma("sp", sem_c[1], lambda e: e.dma_start(out=pv[:], in_=pv_d), writes=[bf("pv")])
        pc1 = [bf("pc1_%d" % i) for i in range(NP1)]
        pc2 = [bf("pc2_%d" % i) for i in range(NP2)]
        sem_si = [S.dma_sem() for _ in range(4)]
        sem_so = [S.dma_sem() for _ in range(6)]
        stage_in = [xT[:, k // 2, (k % 2) * 4:(k % 2) * 4 + 4, :] for k in range(4)]
        stage_out = [actb[:, k * 4:(k + 1) * 4, :] for k in range(6)]
        bsi = [bf("stage_in%d" % k) for k in range(4)]
        bso = [bf("stage_out%d" % k) for k in range(6)]
        S.dma("sp", sem_si[0], lambda e: e.dma_start(out=stage_in[0][:, 0:2, :],
                                                     in_=mask_d.rearrange("p (a b) -> p a b", b=T)), writes=[bsi[0]])
        S.dma("sp", sem_si[1], lambda e: e.dma_start(out=stage_in[1][:, 0:2, :],
                                                     in_=gw_d.rearrange("p (a b) -> p a b", b=T)), writes=[bsi[1]])
        CP("dve", maskt[:, :].rearrange("p (a b) -> p a b", b=T), stage_in[0][:, 0:2, :], [bsi[0]], [bf("mask")])
        CP("dve", gw[:, :].rearrange("p (a b) -> p a b", b=T), stage_in[1][:, 0:2, :], [bsi[1]], [bf("gw")])
        plist = [(w1_d, s1_d, pc1, i) for i in range(NP1)] + [(w2_d, s2_d, pc2, i) for i in range(NP2)]
        if dbg:
            plist = plist[:NP1 + 4]
        castn = [0]

        def cast_piece(n, wd, sd, pcs, i, lazy):
            ki = (2 + n % 2) if lazy else (n % 2)
            ko = n % 6
            S.dma("sp", sem_si[ki], lambda e: e.dma_start(
                out=stage_in[ki], in_=wd[i].rearrange("p (a b) -> p a b", b=T)), writes=[bsi[ki]])
            if n % 2 == 0:
                CP("dve", stage_out[ko], stage_in[ki], [bsi[ki]], [bso[ko]])
            else:
                ACT(stage_out[ko], stage_in[ki], AF.Copy, [bsi[ki]], [bso[ko]])
            S.dma("act", sem_so[ko], lambda e: e.dma_start(
                out=sd[i].rearrange("p (a b) -> p a b", b=T), in_=stage_out[ko]), reads=[bso[ko]], writes=[pcs[i]])

        def cast_gen():
            for n, (wd, sd, pcs, i) in enumerate(plist):
                if n < NP1:
                    continue
                cast_piece(n, wd, sd, pcs, i, True)
                yield

        for n, (wd, sd, pcs, i) in enumerate(plist[:NP1]):
            cast_piece(n, wd, sd, pcs, i, False)
        for n, (wd, sd, pcs, i) in enumerate([]):
            ki, ko = n % 4, n % 6
            S.dma("sp", sem_si[ki], lambda e, wd=wd, i=i, ki=ki: e.dma_start(
                out=stage_in[ki], in_=wd[i].rearrange("p (a b) -> p a b", b=T)), writes=[bsi[ki]])
            if n % 2 == 0:
                CP("dve", stage_out[ko], stage_in[ki], [bsi[ki]], [bso[ko]])
            else:
                ACT(stage_out[ko], stage_in[ki], AF.Copy, [bsi[ki]], [bso[ko]])
            S.dma("act", sem_so[ko], lambda e, sd=sd, i=i, ko=ko: e.dma_start(
                out=sd[i].rearrange("p (a b) -> p a b", b=T), in_=stage_out[ko]), reads=[bso[ko]], writes=[pcs[i]])
        for k in range(4):
            pass
        cst = sb("cst", [128, 4], F32)
        MS("dve", cst[:, 0:1], 0.5, [bf("cst")])
        MS("dve", cst[:, 1:2], -0.5, [bf("cst")])
        MS("dve", cst[:, 2:3], 0.25, [bf("cst")])
        MS("dve", cst[:, 3:4], EPS, [bf("cst")])
        MS("dve", ones64[:], 1.0, [bf("ones64")])
        MS("dve", onesm[:], 1.0 / 1024.0, [bf("onesm")])
        MS("dve", xr[:], 0.0, [bf("xr%d" % c) for c in range(4)])
        MS("dve", halo[:], 0.0, [bf("halo")])
        MS("dve", hst[:], 0.0, [bf("hst")])
        MS("dve", kT[:], 0.0, [bf("kT")])
        MS("dve", Vt[:], 0.0, [bf("Vt")])
        MS("dve", graw[:], 0.0, [bf("graw0"), bf("graw1")])
        bpv, bdv = bf("pv"), bf("dv")
        TS("dve", dv[:, 0:4], pv[:, 20:24], 0.5, None, ALU.mult, ..., [bpv], [bdv])
        TS("dve", dv[:, 4:8], pv[:, 24:28], 0.5, None, ALU.mult, ..., [bpv], [bdv])
        ACT(dv[:, 44:48], pv[:, 28:32], AF.Exp, [bpv], [bdv], scale=-1.0)
        TS("dve", dv[:, 40:44], dv[:, 44:48], -0.25, 1.0 / 3.0, ALU.mult, ALU.add, [bdv], [bdv])
        TT("dve", dv[:, 40:44], dv[:, 40:44], dv[:, 44:48], ALU.mult, [bdv], [bdv])
        TS("dve", dv[:, 40:44], dv[:, 40:44], -0.5, None, ALU.add, ..., [bdv], [bdv])
        TT("dve", dv[:, 40:44], dv[:, 40:44], dv[:, 44:48], ALU.mult, [bdv], [bdv])
        TS("dve", dv[:, 40:44], dv[:, 40:44], 1.0, None, ALU.add, ..., [bdv], [bdv])
        TT("dve", dv[:, 40:44], dv[:, 40:44], dv[:, 44:48], ALU.mult, [bdv], [bdv])
        TS("dve", dv[:, 8:12], dv[:, 40:44], -8.0, None, ALU.mult, ..., [bdv], [bdv])
        TS("dve", dv[:, 12:16], dv[:, 40:44], -4.0, None, ALU.mult, ..., [bdv], [bdv])
        ACT(dv[:, 16:20], pv[:, 168:172], AF.Exp, [bpv], [bdv])
        TS("dve", dv[:, 20:28], pv[:, 128:136], 0.5, None, ALU.mult, ..., [bpv], [bdv])
        TS("dve", dv[:, 28:36], pv[:, 136:144], ALPHA, None, ALU.mult, ..., [bpv], [bdv])
        TS("dve", dv[:, 36:40], pv[:, 144:148], ALPHA, None, ALU.mult, ..., [bpv], [bdv])
        ab1 = sb("ab1", [128, 8], F32)
        TS("dve", ab1[:, 0:8], pv[:, 144:152], ALPHA, None, ALU.mult, ..., [bpv], [bf("ab1")])

        class Stream:
            def __init__(self, ring, nslots, sems, scr, pcs, npieces, eng, name):
                self.ring, self.nslots, self.sems, self.scr, self.pcs = ring, nslots, sems, scr, pcs
                self.npieces, self.eng, self.name = npieces, eng, name
                self.total = npieces * ntiles
                self.loaded = 0
                self.pos = 0
                self.slotbuf = [bf("%s_slot%d" % (name, i)) for i in range(nslots)]

            def _load(self, seq):
                slot = seq % self.nslots
                piece = seq % self.npieces
                dst = self.ring[:, slot, :]
                src = self.scr[piece]
                S.dma(self.eng, self.sems[slot], lambda e: e.dma_start(out=dst, in_=src),
                      reads=[self.pcs[piece]], writes=[self.slotbuf[slot]])

            def prefetch(self, upto):
                while self.loaded < min(upto, self.total):
                    self._load(self.loaded)
                    self.loaded += 1

            def blk(self):
                seq, b = divmod(self.pos, 16)
                self.prefetch(seq + self.nslots)
                self.pos += 1
                slot = seq % self.nslots
                return self.ring[:, slot, b * 128:(b + 1) * 128], self.slotbuf[slot]

        st1 = Stream(ring1, R1, sem_r1, s1_d, pc1, NP1, "sp", "r1")
        st2 = Stream(ring2, R2, sem_r2, s2_d, pc2, NP2, "sp", "r2")

        xrot = [0]
        rot3 = [0]
        rot = {"rnn": 0, "exp": 0, "z": 0, "g": 0, "sg": 0, "ot": 0}

        def nxt(k):
            v = rot[k]
            rot[k] = (v + 1) % 2
            return v

        out_evs = []

        xdone = set()

        def xload(s, blk):
            if (s, blk) in xdone or s >= ntiles:
                return
            xdone.add((s, blk))
            r = blk % 2
            src = x_d[s * T + blk * 128: s * T + (blk + 1) * 128, :]
            S.dma("sp", sem_x[r], lambda e: e.dma_start(out=xraw[:, r, :], in_=src), writes=[bf("xraw%d" % r)])

        pdone = set()

        def pload(s):
            if s in pdone or s >= ntiles:
                return
            pdone.add(s)
            src = p_d[s * T:(s + 1) * T, :].rearrange("(n p) f -> p n f", p=128)
            S.dma("sp", sem_p, lambda e: e.dma_start(out=praw[:, :, :], in_=src), writes=[bf("praw")])

        def MA_a(s):
            t0 = s * T
            xb = s % 2
            bxT = [bf("xT%d_%d" % (xb, c)) for c in range(8)]
            bhT = bf("hT")
            if s < 2:
                for c in range(8):
                    kk_ = xb * 2 + c // 4
                    bxT[c].r.extend(bsi[kk_].r)
                    if bsi[kk_].w is not None:
                        bxT[c].r.append(bsi[kk_].w)
            xload(s, 0)
            xload(s, 1)
            for blk in range(4):
                r = blk % 2
                bx = bf("xraw%d" % r)
                for half in range(2):
                    bk = balloc()
                    for k4 in range(4):
                        kc = half * 4 + k4
                        TR(psb[bk][:, k4 * 128:(k4 + 1) * 128], xraw[:, r, kc * 128:(kc + 1) * 128],
                           [bx], [bank[bk]], inc=(k4 == 3))
                    src_ps = psb[bk][:, :].rearrange("p (a b) -> p a b", b=128)
                    ACT(xT[:, xb, half * 4:half * 4 + 4, blk * 128:(blk + 1) * 128], src_ps, AF.Copy,
                        [bank[bk]], bxT[half * 4:half * 4 + 4])
                    CP("dve", hT[:, half * 4:half * 4 + 4, blk * 128:(blk + 1) * 128],
                       xT[:, xb, half * 4:half * 4 + 4, blk * 128:(blk + 1) * 128],
                       bxT[half * 4:half * 4 + 4], [bhT])
                    bfree(bk)
                if blk + 2 < 4:
                    xload(s, blk + 2)
                yield
            for m in range(14):
                bk = balloc()
                for kc in range(8):
                    w, wb = st1.blk()
                    MM(psb[bk][:, :], w, hT[:, kc, :], kc == 0, kc == 7, [wb, bhT], [bank[bk]], inc=(kc == 7))
                if m < 4:
                    ACT(qT[:, m, :], psb[bk][:, :], AF.Copy, [bank[bk]], [bf("qT")], scale=0.125)
                elif m < 6:
                    CP("dve", kT[:, m - 4, 128:640], psb[bk][:, :], [bank[bk]], [bf("kT")])
                elif m < 10:
                    c = m - 6
                    ACT(xr[:, c, 3:515], psb[bk][:, :], AF.Copy, [bank[bk]], [bf("xr%d" % c)])
                else:
                    c = m - 10
                    ACT(gg[:, c, :], psb[bk][:, :], AF.Gelu_apprx_tanh, [bank[bk]], [bf("gg%d" % c)])
                bfree(bk)
                yield
            bk = balloc()
            vblocks = [st1.blk() for _ in range(8)]
            for _ in range(8):
                st1.blk()
            for blk in range(4):
                for kc in range(8):
                    w, wb = vblocks[kc]
                    MM(psb[bk][:, blk * 128:(blk + 1) * 128], hT[:, kc, blk * 128:(blk + 1) * 128], w,
                       kc == 0, kc == 7, [wb, bhT], [bank[bk]], inc=(kc == 7))
            CP("dve", Vt[:, 128:640], psb[bk][:, :], [bank[bk]], [bf("Vt")])
            bfree(bk)
            yield

        def attn_a(s, Q, g):
            Qg = 4 * s + Q
            kbs = [1] if Qg == 0 else [0, 1]
            bq, bk_ = bf("qT"), bf("kT")
            sc = [balloc(), balloc()]
            for hf in range(2):
                n_mm = len(kbs) * 2
                i_mm = 0
                for kb in kbs:
                    for h2 in range(2):
                        hh = h2 * 2 + hf
                        chunk = 2 * g + hh // 2
                        i_mm += 1
                        MM(psb[sc[hf]][:, (kb * 2 + h2) * 128:(kb * 2 + h2 + 1) * 128],
                           kT[hf * 64:(hf + 1) * 64, g, (Q + kb) * 128:(Q + kb + 1) * 128],
                           qT[hf * 64:(hf + 1) * 64, chunk, Q * 128:(Q + 1) * 128],
                           True, True, [bq, bk_], [bank[sc[hf]]], inc=(i_mm == n_mm))
            r = nxt("exp")
            bex = bf("expT%d" % r)
            c0 = 256 if Qg == 0 else 0
            for hf in range(2):
                ACT(expT[:, r, hf * 512 + c0:(hf + 1) * 512], psb[sc[hf]][:, c0:512], AF.Exp, [bank[sc[hf]]], [bex])
                bfree(sc[hf])
            for hf in range(2):
                TT("dve", expT[:, r, hf * 512 + c0:(hf + 1) * 512], expT[:, r, hf * 512 + c0:(hf + 1) * 512],
                   maskt[:, hf * 512 + c0:(hf + 1) * 512], ALU.mult, [bex, bf("mask")], [bex])
            return (r, kbs)

        def attn_b(s, Q, g, state):
            r, kbs = state
            bv, bmix, bex = bf("Vt"), bf("mixT"), bf("expT%d" % r)
            pvb = balloc()
            for hh in range(4):
                cc = hh // 2
                hf = hh % 2
                for which in range(2):
                    col = which * 256 + cc * 128
                    for ki, kb in enumerate(kbs):
                        lhsT = (Vt[:, (Q + kb) * 128 + g * 64:(Q + kb) * 128 + g * 64 + 64]
                                if which == 0 else ones64[:, :])
                        last = (hh == 3 and which == 1 and ki == len(kbs) - 1)
                        ecol = hf * 512 + (kb * 2 + cc) * 128
                        MM(psb[pvb][hf * 64:(hf + 1) * 64, col:col + 128], lhsT,
                           expT[:, r, ecol:ecol + 128],
                           ki == 0, ki == len(kbs) - 1,
                           [bv, bex, bf("ones64")], [bank[pvb]], inc=last)
            rd = nxt("rnn")
            brd = bf("rden%d" % rd)
            for cc in range(2):
                chunk = 2 * g + cc
                ACT(rden[:, rd, cc * 128:(cc + 1) * 128], psb[pvb][:, 256 + cc * 128:256 + (cc + 1) * 128],
                    AF.Ln, [bank[pvb], bdv], [brd], bias=dv[:, 16 + chunk:17 + chunk], scale=1.0)
                ACT(rden[:, rd, cc * 128:(cc + 1) * 128], rden[:, rd, cc * 128:(cc + 1) * 128],
                    AF.Exp, [brd], [brd], scale=-1.0)
                TT("dve", mixT[:, chunk, Q * 128:(Q + 1) * 128], psb[pvb][:, cc * 128:(cc + 1) * 128],
                   rden[:, rd, cc * 128:(cc + 1) * 128], ALU.mult, [bank[pvb], brd], [bmix])
            bfree(pvb)

        def rnn_a(s, c):
            r = nxt("g")
            bxr, bxc, bxcb = bf("xr%d" % c), bf("xc%d" % r), bf("xcb%d" % r)
            TS("dve", xc[:, r, :], xr[:, c, 0:512], pv[:, 0 * 4 + c:0 * 4 + c + 1], pv[:, 16 + c:17 + c],
               ALU.mult, ALU.add, [bxr, bpv], [bxc])
            for k in range(1, 4):
                STT("dve", xc[:, r, :], xr[:, c, k:k + 512], pv[:, k * 4 + c:k * 4 + c + 1], xc[:, r, :],
                    ALU.mult, ALU.add, [bxr, bpv, bxc], [bxc])
            CP("dve", xr[:, c, 0:3], xr[:, c, 512:515], [bxr], [bxr])
            ACT(xcb[:, r, :], xc[:, r, :], AF.Copy, [bxc], [bxcb])
            return r

        def rnn_b(s, c, r):
            bxcb = bf("xcb%d" % r)
            brr, bii, baa = bf("rr%d" % r), bf("ii%d" % r), bf("aa%d" % r)
            ba_, bx_ = balloc(), balloc()
            MM(psb[ba_][:, :], gw[:, (c * 2) * 128:(c * 2 + 1) * 128], xcb[:, r, :], True, True,
               [bf("gw"), bxcb], [bank[ba_]], inc=True)
            MM(psb[bx_][:, :], gw[:, (c * 2 + 1) * 128:(c * 2 + 2) * 128], xcb[:, r, :], True, True,
               [bf("gw"), bxcb], [bank[bx_]], inc=True)
            ACT(rr[:, r, :], psb[ba_][:, :], AF.Tanh, [bank[ba_], bdv], [brr], bias=dv[:, c:c + 1], scale=0.5)
            ACT(ii[:, r, :], psb[bx_][:, :], AF.Tanh, [bank[bx_], bdv], [bii], bias=dv[:, 4 + c:5 + c], scale=0.5)
            bfree(ba_)
            bfree(bx_)
            ACT(aa[:, r, :], rr[:, r, :], AF.Exp, [brr, bdv], [baa], bias=dv[:, 12 + c:13 + c],
                scale=dv[:, 12 + c:13 + c])
            ACT(rr[:, r, :], rr[:, r, :], AF.Exp, [brr, bdv], [brr], bias=dv[:, 8 + c:9 + c],
                scale=dv[:, 8 + c:9 + c])
            ACT(rr[:, r, :], rr[:, r, :], AF.Sqrt, [brr, bf("cst")], [brr], bias=cst[:, 2:3], scale=-0.25)

        def rnn_c(s, c, r):
            bxc = bf("xc%d" % r)
            brr, bii, baa, bhs = bf("rr%d" % r), bf("ii%d" % r), bf("aa%d" % r), bf("hs%d" % r)
            bhst, bmix = bf("hst"), bf("mixT")
            STT("dve", ii[:, r, :], ii[:, r, :], 1.0, xc[:, r, :], ALU.add, ALU.mult, [bii, bxc], [bii])
            TT("dve", rr[:, r, :], rr[:, r, :], ii[:, r, :], ALU.mult, [brr, bii], [brr])
            S.op("dve", lambda e: e.tensor_tensor_scan(hs[:, r, :], aa[:, r, :], rr[:, r, :], hst[:, c:c + 1],
                                                       ALU.mult, ALU.add),
                 [baa, brr, bhst], [bhs])
            CP("dve", hst[:, c:c + 1], hs[:, r, 511:512], [bhs], [bhst])
            TT("dve", mixT[:, 4 + c, :], hs[:, r, :], gg[:, c, :], ALU.mult, [bhs, bf("gg%d" % c)], [bmix])

        def MA_b(s):
            xload(s + 1, 0)
            xload(s + 1, 1)
            for Q in range(4):
                r = rnn_a(s, Q)
                yield
                sa0 = attn_a(s, Q, 0)
                yield
                sa1 = attn_a(s, Q, 1)
                yield
                rnn_b(s, Q, r)
                yield
                attn_b(s, Q, 0, sa0)
                yield
                attn_b(s, Q, 1, sa1)
                yield
                rnn_c(s, Q, r)
                yield
            CP("dve", kT[:, :, 0:128], kT[:, :, 512:640], [bf("kT")], [bf("kT")])
            CP("dve", Vt[:, 0:128], Vt[:, 512:640], [bf("Vt")], [bf("Vt")])
            yield

        def ln_stats_prep(zsrc, bz):
            r = nxt("z")
            bzb, bzq = bf("zb%d" % r), bf("zq%d" % r)
            ACT(zb[:, r, :], zsrc, AF.Copy, [bz], [bzb])
            ACT(zq[:, r, :], zsrc, AF.Square, [bz], [bzq])
            return r

        def ln_stats_mm(st, m, r):
            bzb, bzq = bf("zb%d" % r), bf("zq%d" % r)
            MM(psb[st[0]][:, :], onesm[:, :], zb[:, r, :], m == 0, m == 7, [bf("onesm"), bzb], [bank[st[0]]],
               inc=(m == 7))
            MM(psb[st[1]][:, :], onesm[:, :], zq[:, r, :], m == 0, m == 7, [bf("onesm"), bzq], [bank[st[1]]],
               inc=(m == 7))

        def ln_finish(st):
            bl = bf("lnm")
            ACT(lnm[:, 2, :], psb[st[0]][:, :], AF.Copy, [bank[st[0]]], [bl])
            TT("dve", lnm[:, 0, :], lnm[:, 2, :], lnm[:, 2, :], ALU.mult, [bl], [bl])
            STT("dve", lnm[:, 0, :], lnm[:, 0, :], -1.0, psb[st[1]][:, :], ALU.mult, ALU.add,
                [bl, bank[st[1]]], [bl])
            ACT(lnm[:, 0, :], lnm[:, 0, :], AF.Ln, [bl, bf("cst")], [bl], bias=cst[:, 3:4], scale=1.0)
            ACT(lnm[:, 0, :], lnm[:, 0, :], AF.Exp, [bl], [bl], scale=-0.5)
            STT("dve", lnm[:, 1, :], lnm[:, 2, :], -1.0, lnm[:, 0, :], ALU.mult, ALU.mult, [bl], [bl])
            bfree(st[0])
            bfree(st[1])

        def MC(s):
            xb = s % 2
            bxT = [bf("xT%d_%d" % (xb, c)) for c in range(8)]
            bmix, bl, bh1 = bf("mixT"), bf("lnm"), bf("h1T")
            pload(s)
            st = (balloc(), balloc())
            for m in range(8):
                bk = balloc()
                for kc in range(8):
                    w, wb = st2.blk()
                    MM(psb[bk][:, :], w, mixT[:, kc, :], kc == 0, kc == 7, [wb, bmix], [bank[bk]], inc=(kc == 7))
                STT("dve", xT[:, xb, m, :], xT[:, xb, m, :], ALPHA, psb[bk][:, :], ALU.mult, ALU.add,
                    [bxT[m], bank[bk]], [bxT[m]])
                bfree(bk)
                if m > 0:
                    ln_stats_mm(st, m - 1, rprev)
                rprev = ln_stats_prep(xT[:, xb, m, :], bxT[m])
                yield
            ln_stats_mm(st, 7, rprev)
            ln_finish(st)
            yield
            for m in range(8):
                TT("dve", xT[:, xb, m, :], xT[:, xb, m, :], lnm[:, 0, :], ALU.mult, [bxT[m], bl], [bxT[m]])
                TT("dve", xT[:, xb, m, :], xT[:, xb, m, :], lnm[:, 1, :], ALU.add, [bxT[m], bl], [bxT[m]])
                ACT(h1T[:, m, :], xT[:, xb, m, :], AF.Identity, [bxT[m], bpv], [bh1],
                    bias=pv[:, 144 + m:145 + m], scale=pv[:, 136 + m:137 + m])
                ACT(xT[:, xb, m, :], xT[:, xb, m, :], AF.Identity, [bxT[m], bdv, bf("ab1")], [bxT[m]],
                    bias=ab1[:, m:m + 1], scale=dv[:, 28 + m:29 + m])
                yield

        def F_up(s):
            bh1, bact, bhalo = bf("h1T"), bf("actb"), bf("halo")
            if s == 0:
                for k in range(6):
                    bact.r.extend(bso[k].r)
                    if bso[k].w is not None:
                        bact.r.append(bso[k].w)

            def stage_b(j, ry, bv_):
                byy = bf("yy%d" % ry)
                ACT(yy[:, ry, :], yy[:, ry, :], AF.Gelu_apprx_tanh, [byy], [byy])
                TT("dve", actb[:, j, :], yy[:, ry, :], psb[bv_][:, :], ALU.mult, [byy, bank[bv_]], [bact])
                bfree(bv_)

            pend = None
            for j in range(24):
                bg_, bv_ = balloc(), balloc()
                for kc in range(8):
                    w, wb = st2.blk()
                    MM(psb[bg_][:, :], w, h1T[:, kc, :], kc == 0, kc == 7, [wb, bh1], [bank[bg_]], inc=(kc == 7))
                for kc in range(8):
                    w, wb = st2.blk()
                    MM(psb[bv_][:, :], w, h1T[:, kc, :], kc == 0, kc == 7, [wb, bh1], [bank[bv_]], inc=(kc == 7))
                r = nxt("g")
                ry = rot3[0]
                rot3[0] = (ry + 1) % 3
                bgr, byy = bf("graw%d" % r), bf("yy%d" % ry)
                CP("dve", graw[:, r, 0:2], halo[:, j, :], [bhalo], [bgr])
                ACT(graw[:, r, 2:514], psb[bg_][:, :], AF.Copy, [bank[bg_]], [bgr])
                ACT(halo[:, j, :], psb[bg_][:, 510:512], AF.Copy, [bank[bg_]], [bhalo])
                ACT(yy[:, ry, :], psb[bg_][:, :], AF.Identity, [bank[bg_], bpv], [byy],
                    bias=pv[:, 104 + j:105 + j], scale=pv[:, 32 + 2 * 24 + j:33 + 2 * 24 + j])
                bfree(bg_)
                for k in range(0, 2):
                    STT("dve", yy[:, ry, :], graw[:, r, k:k + 512], pv[:, 32 + k * 24 + j:33 + k * 24 + j],
                        yy[:, ry, :], ALU.mult, ALU.add, [bgr, bpv, byy], [byy])
                if pend is not None:
                    stage_b(*pend)
                pend = (j, ry, bv_)
                yield
            stage_b(*pend)
            yield

        def F_down(s):
            t0 = s * T
            xb = s % 2
            bxT = [bf("xT%d_%d" % (xb, c)) for c in range(8)]
            bh1, bact, bl, bpr, bpT = bf("h1T"), bf("actb"), bf("lnm"), bf("praw"), bf("pT")
            pload(s)
            for k2 in range(2):
                bk = balloc()
                for blk in range(4):
                    TR(psb[bk][:, blk * 128:(blk + 1) * 128], praw[:, blk, k2 * 128:(k2 + 1) * 128],
                       [bpr], [bank[bk]], inc=(blk == 3))
                CP("dve", pT[:, k2, :], psb[bk][:, :], [bank[bk]], [bpT])
                bfree(bk)
            yield
            for m in range(8):
                bg_, bp_ = balloc(), balloc()
                for kc in range(8):
                    w, wb = st2.blk()
                    MM(psb[bg_][:, :], w, h1T[:, kc, :], kc == 0, kc == 7, [wb, bh1], [bank[bg_]], inc=(kc == 7))
                for k2 in range(2):
                    w, wb = st2.blk()
                    MM(psb[bp_][:, :], w, pT[:, k2, :], k2 == 0, k2 == 1, [wb, bpT], [bank[bp_]], inc=(k2 == 1))
                r = nxt("sg")
                bsg = bf("sg%d" % r)
                ACT(sg[:, r, :], psb[bg_][:, :], AF.Tanh, [bank[bg_], bdv], [bsg], bias=dv[:, 20 + m:21 + m],
                    scale=0.5)
                bfree(bg_)
                STT("dve", sg[:, r, :], sg[:, r, :], 1.0, psb[bp_][:, :], ALU.add, ALU.mult,
                    [bsg, bank[bp_]], [bsg])
                bfree(bp_)
                STT("dve", xT[:, xb, m, :], sg[:, r, :], 0.5, xT[:, xb, m, :], ALU.mult, ALU.add,
                    [bsg, bxT[m]], [bxT[m]])
                yield
            st = (balloc(), balloc())
            for m in range(8):
                bk = balloc()
                for c in range(24):
                    w, wb = st2.blk()
                    MM(psb[bk][:, :], w, actb[:, c, :], c == 0, c == 23, [wb, bact], [bank[bk]], inc=(c == 23))
                TT("dve", xT[:, xb, m, :], xT[:, xb, m, :], psb[bk][:, :], ALU.add, [bxT[m], bank[bk]], [bxT[m]])
                bfree(bk)
                if m > 0:
                    ln_stats_mm(st, m - 1, rprev)
                rprev = ln_stats_prep(xT[:, xb, m, :], bxT[m])
                yield
            ln_stats_mm(st, 7, rprev)
            ln_finish(st)
            yield
            for m in range(8):
                TT("dve", xT[:, xb, m, :], xT[:, xb, m, :], lnm[:, 0, :], ALU.mult, [bxT[m], bl], [bxT[m]])
                TT("dve", xT[:, xb, m, :], xT[:, xb, m, :], lnm[:, 1, :], ALU.add, [bxT[m], bl], [bxT[m]])
                ACT(xT[:, xb, m, :], xT[:, xb, m, :], AF.Identity, [bxT[m], bpv], [bxT[m]],
                    bias=pv[:, 160 + m:161 + m], scale=pv[:, 152 + m:153 + m])
                yield
            for blk in range(4):
                r = 0
                ro = nxt("ot")
                bot = bf("ot%d" % r)
                for half in range(2):
                    bk = balloc()
                    for k4 in range(4):
                        m = half * 4 + k4
                        TR(psb[bk][:, k4 * 128:(k4 + 1) * 128], xT[:, xb, m, blk * 128:(blk + 1) * 128],
                           [bxT[m]], [bank[bk]], inc=(k4 == 3))
                    if half == 0:
                        ACT(ot[:, r, 0:512], psb[bk][:, :], AF.Copy, [bank[bk]], [bot])
                    else:
                        CP("dve", ot[:, r, 512:1024], psb[bk][:, :], [bank[bk]], [bot])
                    bfree(bk)
                dst = out_d[t0 + blk * 128:t0 + (blk + 1) * 128, :]
                ev = S.dma("sp", sem_o[ro], lambda e, r=r, dst=dst: e.dma_start(out=dst, in_=ot[:, r, :]),
                           reads=[bot], writes=[bf("outd%d" % ro)])
                out_evs.append(ev)
                yield

        if dbg == 0:
            cg = cast_gen()
            for s in range(ntiles + 1):
                g1 = MA_a(s) if s < ntiles else None
                g2 = chain(MC(s - 1), F_up(s - 1)) if s >= 1 else None
                if s == 0:
                    interleave(g1, take(cg, 22))
                else:
                    interleave_pat(g1, g2, "ABABABABBBBBB" + "AABAABAAB" + "ABABABABAB")
                g3 = MA_b(s) if s < ntiles else None
                g4 = F_down(s - 1) if s >= 1 else None
                if s == 0:
                    interleave(g3, cg)
                else:
                    interleave_pat(g3, g4, "AAB" * 15)
            assert st1.pos == 128 * ntiles and st2.pos == 720 * ntiles, (st1.pos, st2.pos)
        else:
            interleave(cast_gen())
            stages = [MA_a, MA_b, MC, F_up, F_down]
            for st_ in stages[:dbg - 1]:
                interleave(st_(0))
        S.wait_events("sp", [e_ for e_ in out_evs[-2:] if e_ is not None] + [S.last[k] for k in sem_so if k in S.last])
        if dbg or limit:
            S.drain("sp")
        S.emit()
    return nc


def _blocks_to_pieces(blocks):
    n = len(blocks)
    assert n % 16 == 0
    arr = np.stack(blocks, 0).reshape(n // 16, 16, 128, 128)
    return np.ascontiguousarray(arr.transpose(0, 2, 1, 3).reshape(n // 16, 128, 2048))


def prep_weights(w_in, w_out, w_ffn_up, w_ffn_down, ple_gate_w, ple_proj):
    w_in, w_out, w_up, w_dn, wg, wp = (np.asarray(a[0], np.float32) for a in
                                       (w_in, w_out, w_ffn_up, w_ffn_down, ple_gate_w, ple_proj))
    cols = [w_in[:, 0:512],
            w_in[:, 512:576], w_in[:, 512:576],
            w_in[:, 576:640], w_in[:, 576:640],
            w_in[:, 768:1280], w_in[:, 1280:1792], w_in[:, 640:768],
            np.zeros((1024, 128), np.float32)]
    wi = np.concatenate(cols, axis=1)
    assert wi.shape == (1024, 2048)
    b1 = [wi[kc * 128:(kc + 1) * 128, m * 128:(m + 1) * 128] for m in range(16) for kc in range(8)]
    w1 = _blocks_to_pieces(b1)
    b2 = []
    for m in range(8):
        for kc in range(8):
            b2.append(w_out[kc * 128:(kc + 1) * 128, m * 128:(m + 1) * 128])
    for j in range(24):
        for kc in range(8):
            b2.append(w_up[kc * 128:(kc + 1) * 128, j * 128:(j + 1) * 128])
        for kc in range(8):
            b2.append(w_up[kc * 128:(kc + 1) * 128, 3072 + j * 128:3072 + (j + 1) * 128])
    for m in range(8):
        for kc in range(8):
            b2.append(wg[kc * 128:(kc + 1) * 128, m * 128:(m + 1) * 128])
        for k2 in range(2):
            b2.append(wp[k2 * 128:(k2 + 1) * 128, m * 128:(m + 1) * 128])
    for m in range(8):
        for c in range(24):
            b2.append(w_dn[c * 128:(c + 1) * 128, m * 128:(m + 1) * 128])
    w2 = _blocks_to_pieces(b2)
    assert w1.shape[0] == NP1 and w2.shape[0] == NP2
    return w1, w2


def prep_small(attn_sinks, rnn_conv_w, rnn_conv_b, gate_a_w, gate_a_b, gate_x_w, gate_x_b, lru_lambda,
               ln1_g, ln1_b, ffn_conv_w, ffn_conv_b, ple_gate_b, ln2_g, ln2_b):
    f = lambda a: np.asarray(a[0], np.float32)
    pv = np.zeros((128, NPV), np.float32)
    cw = f(rnn_conv_w)
    for k in range(4):
        pv[:, k * 4:(k + 1) * 4] = cw[k].reshape(4, 128).T
    pv[:, 16:20] = f(rnn_conv_b).reshape(4, 128).T
    pv[:, 20:24] = f(gate_a_b).reshape(4, 128).T
    pv[:, 24:28] = f(gate_x_b).reshape(4, 128).T
    pv[:, 28:32] = f(lru_lambda).reshape(4, 128).T
    fw = f(ffn_conv_w)
    for k in range(3):
        pv[:, 32 + k * 24:32 + (k + 1) * 24] = fw[k].reshape(24, 128).T
    pv[:, 104:128] = f(ffn_conv_b).reshape(24, 128).T
    pv[:, 128:136] = f(ple_gate_b).reshape(8, 128).T
    pv[:, 136:144] = f(ln1_g).reshape(8, 128).T
    pv[:, 144:152] = f(ln1_b).reshape(8, 128).T
    pv[:, 152:160] = f(ln2_g).reshape(8, 128).T
    pv[:, 160:168] = f(ln2_b).reshape(8, 128).T
    sk = f(attn_sinks)
    for m in range(4):
        pv[0:64, 168 + m] = sk[2 * m]
        pv[64:128, 168 + m] = sk[2 * m + 1]
    gw = np.zeros((128, 1024), np.float32)
    for c in range(4):
        for gi, wsrc in enumerate((f(gate_a_w), f(gate_x_w))):
            blk = np.zeros((128, 128), np.float32)
            blk[0:64, 0:64] = wsrc[2 * c]
            blk[64:128, 64:128] = wsrc[2 * c + 1]
            gw[:, (c * 2 + gi) * 128:(c * 2 + gi + 1) * 128] = blk
    s_idx = np.arange(128)[:, None]
    q_idx = np.arange(128)[None, :]
    mprev = (s_idx > q_idx).astype(np.float32)
    mcur = (s_idx <= q_idx).astype(np.float32)
    mask = np.concatenate([mprev, mprev, mcur, mcur, mprev, mprev, mcur, mcur], axis=1)
    ident = np.eye(128, dtype=np.float32)
    return pv, gw, np.ascontiguousarray(mask), ident


_NC_CACHE = {}


def run(inputs, NT, trace=False):
    x = np.asarray(inputs["x"], np.float32)
    p = np.asarray(inputs["p"], np.float32)[0]
    w1, w2 = prep_weights(inputs["w_in"], inputs["w_out"], inputs["w_ffn_up"], inputs["w_ffn_down"],
                          inputs["ple_gate_w"], inputs["ple_proj"])
    pv, gw, mask, ident = prep_small(inputs["attn_sinks"], inputs["rnn_conv_w"], inputs["rnn_conv_b"],
                                     inputs["gate_a_w"], inputs["gate_a_b"], inputs["gate_x_w"],
                                     inputs["gate_x_b"], inputs["lru_lambda"], inputs["ln1_g"], inputs["ln1_b"],
                                     inputs["ffn_conv_w"], inputs["ffn_conv_b"], inputs["ple_gate_b"],
                                     inputs["ln2_g"], inputs["ln2_b"])
    if NT not in _NC_CACHE:
        _NC_CACHE[NT] = build(NT)
    nc = _NC_CACHE[NT]
    in_maps = []
    for b in range(NB):
        in_maps.append({"x": np.ascontiguousarray(x[b, :NT]), "p": np.ascontiguousarray(p[b, :NT]),
                        "w1": w1, "w2": w2, "gw": gw, "mask": mask, "ident": ident, "pv": pv})
    res = run_bass_kernel_spmd(nc, in_maps, core_ids=list(range(NB)), trace=trace)
    out = np.stack([r["out"] for r in res.results], 0)
    return out, res


def kernel(**inputs):
    out, _ = run(inputs, SEQ)
    return out.astype(np.float32)
```

```python
from contextlib import ExitStack
import numpy as np
import concourse.bass as bass
import concourse.mybir as mybir
from concourse.bass_utils import run_bass_kernel_spmd

F32 = mybir.dt.float32
BF16 = mybir.dt.bfloat16
AF = mybir.ActivationFunctionType
ALU = mybir.AluOpType

D = 1024
SEQ = 8192
NB = 8
T = 512
ALPHA = float(2.0 ** 0.25)
EPS = 1e-5
R1 = 2
R2 = 4
NP1 = 8
NP2 = 45
NPV = 172
ENGS = ("pe", "act", "dve", "pool", "sp")


class Buf:
    __slots__ = ("name", "w", "r", "excl")

    def __init__(self, name, excl=False):
        self.name = name
        self.w = None
        self.r = []
        self.excl = excl


class Ev:
    __slots__ = ("key", "val", "eng", "snap")

    def __init__(self, key, val, eng, snap):
        self.key = key
        self.val = val
        self.eng = eng
        self.snap = snap


class Sched:
    def __init__(self, nc, ctx):
        self.nc = nc
        self.ctx = ctx
        self.ops = {e: [] for e in ENGS}
        self.cnt = {}
        self.seen = {e: {} for e in ENGS}
        self.sems = {}
        self.last = {}
        for e in ENGS:
            if e != "sp":
                self.sems[e] = ctx.enter_context(nc.semaphore("s_" + e))
                self.cnt[e] = 0
        self.n_dma_sem = 0
        self.pending = {e: False for e in ENGS}
        self.total = 0
        self.limit = 1 << 60

    def dma_sem(self):
        self.n_dma_sem += 1
        key = "dma%d" % self.n_dma_sem
        self.sems[key] = self.ctx.enter_context(self.nc.semaphore(key))
        self.cnt[key] = 0
        return key

    def _need(self, eng, reads, writes, extra=()):
        need = {}

        def req(ev, kind):
            if ev is None:
                return
            if ev.key == eng:
                if eng == "pe":
                    return
                if kind == "war":
                    return
            if need.get(ev.key) is None or need[ev.key].val < ev.val:
                need[ev.key] = ev

        for b in reads:
            req(b.w, "raw")
            if b.excl:
                for r in b.r:
                    if r.key != eng:
                        req(r, "raw")
        for b in writes:
            req(b.w, "waw")
            for r in b.r:
                req(r, "war")
        for ev in extra:
            req(ev, "raw")
        seen = self.seen[eng]
        waits = []
        for k, ev in need.items():
            if seen.get(k, 0) < ev.val:
                waits.append((k, ev.val))
        for k, ev in need.items():
            if seen.get(k, 0) < ev.val:
                seen[k] = ev.val
            for k2, v2 in ev.snap.items():
                if seen.get(k2, 0) < v2:
                    seen[k2] = v2
        return waits

    def _commit(self, ev, reads, writes):
        for b in reads:
            b.r.append(ev)
            if len(b.r) > 64:
                best = {}
                for r in b.r:
                    if best.get(r.key) is None or best[r.key].val < r.val:
                        best[r.key] = r
                b.r = list(best.values())
        for b in writes:
            b.w = ev
            b.r = []

    def op(self, eng, fn, reads=(), writes=(), inc=True):
        self.total += 1
        if self.total > self.limit:
            return None
        waits = self._need(eng, reads, writes)
        if inc:
            self.cnt[eng] += 1
            val = self.cnt[eng]
            self.pending[eng] = False
        else:
            val = self.cnt[eng] + 1
            self.pending[eng] = True
        ev = Ev(eng, val, eng, dict(self.seen[eng]))
        self._commit(ev, reads, writes)
        self.ops[eng].append((waits, fn, eng if inc else None, 1))
        return ev

    def dma(self, eng, semkey, fn, reads=(), writes=()):
        self.total += 1
        if self.total > self.limit:
            return None
        extra = (self.last[semkey],) if semkey in self.last else ()
        waits = self._need(eng, reads, writes, extra)
        self.cnt[semkey] += 16
        ev = Ev(semkey, self.cnt[semkey], eng, dict(self.seen[eng]))
        self.last[semkey] = ev
        self._commit(ev, reads, writes)
        self.ops[eng].append((waits, fn, semkey, 16))
        return ev

    def wait_events(self, eng, evs):
        waits = self._need(eng, (), (), evs)
        self.ops[eng].append((waits, None, None, 0))

    def drain(self, eng):
        waits = []
        for k, v in self.cnt.items():
            if v > 0 and self.seen[eng].get(k, 0) < v:
                waits.append((k, v))
                self.seen[eng][k] = v
        self.ops[eng].append((waits, None, None, 0))

    def emit(self):
        nc = self.nc
        sems = self.sems
        ops = self.ops

        def replay(engh, name):
            for waits, fn, semkey, inc in ops[name]:
                for k, v in waits:
                    engh.wait_ge(sems[k], v)
                if fn is None:
                    continue
                ins = fn(engh)
                if semkey is not None:
                    ins.then_inc(sems[semkey], inc)

        with nc.Block() as block:
            @block.tensor
            def _(e):
                replay(e, "pe")

            @block.scalar
            def _(e):
                replay(e, "act")

            @block.vector
            def _(e):
                replay(e, "dve")

            @block.gpsimd
            def _(e):
                replay(e, "pool")

            @block.sync
            def _(e):
                replay(e, "sp")


MARKS = []
_SREF = [None]


def interleave(*gens):
    gens = [g for g in gens if g is not None]
    while gens:
        for g in list(gens):
            try:
                next(g)
                MARKS.append(_SREF[0].total)
            except StopIteration:
                gens.remove(g)


def interleave_pat(ga, gb, pat):
    gens = {"A": ga, "B": gb}
    alive = {k for k, g in gens.items() if g is not None}
    i = 0
    while alive:
        if i < len(pat):
            k = pat[i]
        else:
            k = "AB"[(i - len(pat)) % 2]
        i += 1
        if k not in alive:
            k = next(iter(alive))
        try:
            next(gens[k])
            MARKS.append(_SREF[0].total)
        except StopIteration:
            alive.discard(k)


def take(gen, n):
    for _ in range(n):
        try:
            next(gen)
        except StopIteration:
            return
        yield


def chain(*gens):
    for g in gens:
        if g is not None:
            yield from g


def build(NT, dbg=0, limit=None):
    ntiles = NT // T
    nc = bass.Bass("TRN2", target_bir_lowering=False)
    x_d = nc.dram_tensor("x", [NT, D], F32, kind="ExternalInput").ap()
    p_d = nc.dram_tensor("p", [NT, 256], F32, kind="ExternalInput").ap()
    w1_d = nc.dram_tensor("w1", [NP1, 128, 2048], F32, kind="ExternalInput").ap()
    w2_d = nc.dram_tensor("w2", [NP2, 128, 2048], F32, kind="ExternalInput").ap()
    gw_d = nc.dram_tensor("gw", [128, 1024], F32, kind="ExternalInput").ap()
    mask_d = nc.dram_tensor("mask", [128, 1024], F32, kind="ExternalInput").ap()
    ident_d = nc.dram_tensor("ident", [128, 128], F32, kind="ExternalInput").ap()
    pv_d = nc.dram_tensor("pv", [128, NPV], F32, kind="ExternalInput").ap()
    out_d = nc.dram_tensor("out", [NT, D], F32, kind="ExternalOutput").ap()
    s1_d = nc.dram_tensor("s1", [NP1, 128, 2048], BF16).ap()
    s2_d = nc.dram_tensor("s2", [NP2, 128, 2048], BF16).ap()

    with ExitStack() as ctx:
        S = Sched(nc, ctx)
        _SREF[0] = S
        MARKS.append(-1)
        if limit:
            S.limit = limit

        def sb(name, shape, dt):
            return ctx.enter_context(nc.sbuf_tensor("sb_" + name, shape, dt))

        ident = sb("ident", [128, 128], F32)
        maskt = sb("maskt", [128, 1024], BF16)
        ones64 = sb("ones64", [128, 64], BF16)
        onesm = sb("onesm", [128, 128], BF16)
        pv = sb("pv", [128, NPV], F32)
        dv = sb("dv", [128, 48], F32)
        gw = sb("gw", [128, 1024], BF16)
        ring1 = sb("ring1", [128, R1, 2048], BF16)
        ring2 = sb("ring2", [128, R2, 2048], BF16)
        xraw = sb("xraw", [128, 2, 1024], F32)
        xT = sb("xT", [128, 2, 8, T], F32)
        hT = sb("hT", [128, 8, T], BF16)
        qT = sb("qT", [128, 4, T], BF16)
        kT = sb("kT", [128, 2, 640], BF16)
        Vt = sb("Vt", [128, 640], BF16)
        xr = sb("xr", [128, 4, 516], F32)
        gg = sb("gg", [128, 4, T], F32)
        xc = sb("xc", [128, 2, T], F32)
        xcb = sb("xcb", [128, 2, T], BF16)
        rr = sb("rr", [128, 2, T], F32)
        ii = sb("ii", [128, 2, T], F32)
        aa = sb("aa", [128, 2, T], F32)
        hs = sb("hs", [128, 2, T], F32)
        hst = sb("hst", [128, 4], F32)
        expT = sb("expT", [128, 2, 1024], BF16)
        rden = sb("rden", [128, 2, 256], F32)
        mixT = sb("mixT", [128, 8, T], BF16)
        h1T = sb("h1T", [128, 8, T], BF16)
        zb = sb("zb", [128, 2, T], BF16)
        zq = sb("zq", [128, 2, T], BF16)
        lnm = sb("lnm", [128, 3, T], F32)
        graw = sb("graw", [128, 2, 516], F32)
        yy = sb("yy", [128, 3, T], F32)
        halo = sb("halo", [128, 24, 2], F32)
        actb = sb("actb", [128, 24, T], BF16)
        praw = sb("praw", [128, 4, 256], F32)
        pT = sb("pT", [128, 2, T], BF16)
        sg = sb("sg", [128, 2, T], F32)
        ot = sb("ot", [128, 1, 1024], F32)
        psb = [ctx.enter_context(nc.psum_tensor("ps%d" % i, [128, 512], F32)) for i in range(8)]

        B = {}

        def bf(name):
            if name not in B:
                B[name] = Buf(name)
            return B[name]

        bank = [bf("bank%d" % i) for i in range(8)]
        for b_ in bank:
            b_.excl = True
        free_banks = list(range(8))

        def balloc():
            assert free_banks, "out of PSUM banks"
            return free_banks.pop(0)

        def bfree(i):
            free_banks.append(i)

        def ACT(out, in_, func, reads, writes, bias=None, scale=None):
            kw = {}
            if bias is not None:
                kw["bias"] = bias
            if scale is not None:
                kw["scale"] = scale
            S.op("act", lambda e: e.activation(out, in_, func, **kw), reads, writes)

        def TT(eng, out, in0, in1, op, reads, writes):
            S.op(eng, lambda e: e.tensor_tensor(out, in0, in1, op), reads, writes)

        def STT(eng, out, in0, scalar, in1, op0, op1, reads, writes):
            S.op(eng, lambda e: e.scalar_tensor_tensor(out, in0, scalar, in1, op0, op1), reads, writes)

        def TS(eng, out, in0, s1, s2, op0, op1, reads, writes):
            if op1 is Ellipsis:
                S.op(eng, lambda e: e.tensor_scalar(out, in0, s1, s2, op0), reads, writes)
            else:
                S.op(eng, lambda e: e.tensor_scalar(out, in0, s1, s2, op0, op1), reads, writes)

        def CP(eng, out, in_, reads, writes):
            S.op(eng, lambda e: e.tensor_copy(out, in_), reads, writes)

        def MM(out, lhsT, rhs, start, stop, reads, writes, inc):
            S.op("pe", lambda e: e.matmul(out, lhsT, rhs, start=start, stop=stop), reads, writes, inc=inc)

        def TR(out, in_, reads, writes, inc):
            S.op("pe", lambda e: e.transpose(out, in_, ident[:]), reads + [bf("ident")], writes, inc=inc)

        def MS(eng, ap, val, writes):
            S.op(eng, lambda e: e.memset(ap, val), (), writes)

        sem_r1 = [S.dma_sem() for _ in range(R1)]
        sem_r2 = [S.dma_sem() for _ in range(R2)]
        sem_x = [S.dma_sem() for _ in range(2)]
        sem_p = S.dma_sem()
        sem_o = [S.dma_sem() for _ in range(2)]
        sem_c = [S.dma_sem() for _ in range(4)]

        S.dma("sp", sem_c[0], lambda e: e.dma_start(out=ident[:], in_=ident_d), writes=[bf("ident")])
        S.dma("sp", sem_c[1], lambda e: e.dma_start(out=pv[:], in_=pv_d), writes=[bf("pv")])
        pc1 = [bf("pc1_%d" % i) for i in range(NP1)]
        pc2 = [bf("pc2_%d" % i) for i in range(NP2)]
        sem_si = [S.dma_sem() for _ in range(4)]
        sem_so = [S.dma_sem() for _ in range(6)]
        stage_in = [xT[:, k // 2, (k % 2) * 4:(k % 2) * 4 + 4, :] for k in range(4)]
        stage_out = [actb[:, k * 4:(k + 1) * 4, :] for k in range(6)]
        bsi = [bf("stage_in%d" % k) for k in range(4)]
        bso = [bf("stage_out%d" % k) for k in range(6)]
        S.dma("sp", sem_si[0], lambda e: e.dma_start(out=stage_in[0][:, 0:2, :],
                                                     in_=mask_d.rearrange("p (a b) -> p a b", b=T)), writes=[bsi[0]])
        S.dma("sp", sem_si[1], lambda e: e.dma_start(out=stage_in[1][:, 0:2, :],
                                                     in_=gw_d.rearrange("p (a b) -> p a b", b=T)), writes=[bsi[1]])
        CP("dve", maskt[:, :].rearrange("p (a b) -> p a b", b=T), stage_in[0][:, 0:2, :], [bsi[0]], [bf("mask")])
        CP("dve", gw[:, :].rearrange("p (a b) -> p a b", b=T), stage_in[1][:, 0:2, :], [bsi[1]], [bf("gw")])
        plist = [(w1_d, s1_d, pc1, i) for i in range(NP1)] + [(w2_d, s2_d, pc2, i) for i in range(NP2)]
        if dbg:
            plist = plist[:NP1 + 4]
        castn = [0]

        def cast_piece(n, wd, sd, pcs, i, lazy):
            ki = (2 + n % 2) if lazy else (n % 2)
            ko = n % 6
            S.dma("sp", sem_si[ki], lambda e: e.dma_start(
                out=stage_in[ki], in_=wd[i].rearrange("p (a b) -> p a b", b=T)), writes=[bsi[ki]])
            if n % 2 == 0:
                CP("dve", stage_out[ko], stage_in[ki], [bsi[ki]], [bso[ko]])
            else:
                ACT(stage_out[ko], stage_in[ki], AF.Copy, [bsi[ki]], [bso[ko]])
            S.dma("act", sem_so[ko], lambda e: e.dma_start(
                out=sd[i].rearrange("p (a b) -> p a b", b=T), in_=stage_out[ko]), reads=[bso[ko]], writes=[pcs[i]])

        def cast_gen():
            for n, (wd, sd, pcs, i) in enumerate(plist):
                if n < NP1:
                    continue
                cast_piece(n, wd, sd, pcs, i, True)
                yield

        for n, (wd, sd, pcs, i) in enumerate(plist[:NP1]):
            cast_piece(n, wd, sd, pcs, i, False)
        for n, (wd, sd, pcs, i) in enumerate([]):
            ki, ko = n % 4, n % 6
            S.dma("sp", sem_si[ki], lambda e, wd=wd, i=i, ki=ki: e.dma_start(
                out=stage_in[ki], in_=wd[i].rearrange("p (a b) -> p a b", b=T)), writes=[bsi[ki]])
            if n % 2 == 0:
                CP("dve", stage_out[ko], stage_in[ki], [bsi[ki]], [bso[ko]])
            else:
                ACT(stage_out[ko], stage_in[ki], AF.Copy, [bsi[ki]], [bso[ko]])
            S.dma("act", sem_so[ko], lambda e, sd=sd, i=i, ko=ko: e.dma_start(
                out=sd[i].rearrange("p (a b) -> p a b", b=T), in_=stage_out[ko]), reads=[bso[ko]], writes=[pcs[i]])
        for k in range(4):
            pass
        cst = sb("cst", [128, 4], F32)
        MS("dve", cst[:, 0:1], 0.5, [bf("cst")])
        MS("dve", cst[:, 1:2], -0.5, [bf("cst")])
        MS("dve", cst[:, 2:3], 0.25, [bf("cst")])
        MS("dve", cst[:, 3:4], EPS, [bf("cst")])
        MS("dve", ones64[:], 1.0, [bf("ones64")])
        MS("dve", onesm[:], 1.0 / 1024.0, [bf("onesm")])
        MS("dve", xr[:], 0.0, [bf("xr%d" % c) for c in range(4)])
        MS("dve", halo[:], 0.0, [bf("halo")])
        MS("dve", hst[:], 0.0, [bf("hst")])
        MS("dve", kT[:], 0.0, [bf("kT")])
        MS("dve", Vt[:], 0.0, [bf("Vt")])
        MS("dve", graw[:], 0.0, [bf("graw0"), bf("graw1")])
        bpv, bdv = bf("pv"), bf("dv")
        TS("dve", dv[:, 0:4], pv[:, 20:24], 0.5, None, ALU.mult, ..., [bpv], [bdv])
        TS("dve", dv[:, 4:8], pv[:, 24:28], 0.5, None, ALU.mult, ..., [bpv], [bdv])
        ACT(dv[:, 44:48], pv[:, 28:32], AF.Exp, [bpv], [bdv], scale=-1.0)
        TS("dve", dv[:, 40:44], dv[:, 44:48], -0.25, 1.0 / 3.0, ALU.mult, ALU.add, [bdv], [bdv])
        TT("dve", dv[:, 40:44], dv[:, 40:44], dv[:, 44:48], ALU.mult, [bdv], [bdv])
        TS("dve", dv[:, 40:44], dv[:, 40:44], -0.5, None, ALU.add, ..., [bdv], [bdv])
        TT("dve", dv[:, 40:44], dv[:, 40:44], dv[:, 44:48], ALU.mult, [bdv], [bdv])
        TS("dve", dv[:, 40:44], dv[:, 40:44], 1.0, None, ALU.add, ..., [bdv], [bdv])
        TT("dve", dv[:, 40:44], dv[:, 40:44], dv[:, 44:48], ALU.mult, [bdv], [bdv])
        TS("dve", dv[:, 8:12], dv[:, 40:44], -8.0, None, ALU.mult, ..., [bdv], [bdv])
        TS("dve", dv[:, 12:16], dv[:, 40:44], -4.0, None, ALU.mult, ..., [bdv], [bdv])
        ACT(dv[:, 16:20], pv[:, 168:172], AF.Exp, [bpv], [bdv])
        TS("dve", dv[:, 20:28], pv[:, 128:136], 0.5, None, ALU.mult, ..., [bpv], [bdv])
        TS("dve", dv[:, 28:36], pv[:, 136:144], ALPHA, None, ALU.mult, ..., [bpv], [bdv])
        TS("dve", dv[:, 36:40], pv[:, 144:148], ALPHA, None, ALU.mult, ..., [bpv], [bdv])
        ab1 = sb("ab1", [128, 8], F32)
        TS("dve", ab1[:, 0:8], pv[:, 144:152], ALPHA, None, ALU.mult, ..., [bpv], [bf("ab1")])

        class Stream:
            def __init__(self, ring, nslots, sems, scr, pcs, npieces, eng, name):
                self.ring, self.nslots, self.sems, self.scr, self.pcs = ring, nslots, sems, scr, pcs
                self.npieces, self.eng, self.name = npieces, eng, name
                self.total = npieces * ntiles
                self.loaded = 0
                self.pos = 0
                self.slotbuf = [bf("%s_slot%d" % (name, i)) for i in range(nslots)]

            def _load(self, seq):
                slot = seq % self.nslots
                piece = seq % self.npieces
                dst = self.ring[:, slot, :]
                src = self.scr[piece]
                S.dma(self.eng, self.sems[slot], lambda e: e.dma_start(out=dst, in_=src),
                      reads=[self.pcs[piece]], writes=[self.slotbuf[slot]])

            def prefetch(self, upto):
                while self.loaded < min(upto, self.total):
                    self._load(self.loaded)
                    self.loaded += 1

            def blk(self):
                seq, b = divmod(self.pos, 16)
                self.prefetch(seq + self.nslots)
                self.pos += 1
                slot = seq % self.nslots
                return self.ring[:, slot, b * 128:(b + 1) * 128], self.slotbuf[slot]

        st1 = Stream(ring1, R1, sem_r1, s1_d, pc1, NP1, "sp", "r1")
        st2 = Stream(ring2, R2, sem_r2, s2_d, pc2, NP2, "sp", "r2")

        xrot = [0]
        rot3 = [0]
        rot = {"rnn": 0, "exp": 0, "z": 0, "g": 0, "sg": 0, "ot": 0}

        def nxt(k):
            v = rot[k]
            rot[k] = (v + 1) % 2
            return v

        out_evs = []

        xdone = set()

        def xload(s, blk):
            if (s, blk) in xdone or s >= ntiles:
                return
            xdone.add((s, blk))
            r = blk % 2
            src = x_d[s * T + blk * 128: s * T + (blk + 1) * 128, :]
            S.dma("sp", sem_x[r], lambda e: e.dma_start(out=xraw[:, r, :], in_=src), writes=[bf("xraw%d" % r)])

        pdone = set()

        def pload(s):
            if s in pdone or s >= ntiles:
                return
            pdone.add(s)
            src = p_d[s * T:(s + 1) * T, :].rearrange("(n p) f -> p n f", p=128)
            S.dma("sp", sem_p, lambda e: e.dma_start(out=praw[:, :, :], in_=src), writes=[bf("praw")])

        def MA_a(s):
            t0 = s * T
            xb = s % 2
            bxT = [bf("xT%d_%d" % (xb, c)) for c in range(8)]
            bhT = bf("hT")
            if s < 2:
                for c in range(8):
                    kk_ = xb * 2 + c // 4
                    bxT[c].r.extend(bsi[kk_].r)
                    if bsi[kk_].w is not None:
                        bxT[c].r.append(bsi[kk_].w)
            xload(s, 0)
            xload(s, 1)
            for blk in range(4):
                r = blk % 2
                bx = bf("xraw%d" % r)
                for half in range(2):
                    bk = balloc()
                    for k4 in range(4):
                        kc = half * 4 + k4
                        TR(psb[bk][:, k4 * 128:(k4 + 1) * 128], xraw[:, r, kc * 128:(kc + 1) * 128],
                           [bx], [bank[bk]], inc=(k4 == 3))
                    src_ps = psb[bk][:, :].rearrange("p (a b) -> p a b", b=128)
                    ACT(xT[:, xb, half * 4:half * 4 + 4, blk * 128:(blk + 1) * 128], src_ps, AF.Copy,
                        [bank[bk]], bxT[half * 4:half * 4 + 4])
                    CP("dve", hT[:, half * 4:half * 4 + 4, blk * 128:(blk + 1) * 128],
                       xT[:, xb, half * 4:half * 4 + 4, blk * 128:(blk + 1) * 128],
                       bxT[half * 4:half * 4 + 4], [bhT])
                    bfree(bk)
                if blk + 2 < 4:
                    xload(s, blk + 2)
                yield
            for m in range(14):
                bk = balloc()
                for kc in range(8):
                    w, wb = st1.blk()
                    MM(psb[bk][:, :], w, hT[:, kc, :], kc == 0, kc == 7, [wb, bhT], [bank[bk]], inc=(kc == 7))
                if m < 4:
                    ACT(qT[:, m, :], psb[bk][:, :], AF.Copy, [bank[bk]], [bf("qT")], scale=0.125)
                elif m < 6:
                    CP("dve", kT[:, m - 4, 128:640], psb[bk][:, :], [bank[bk]], [bf("kT")])
                elif m < 10:
                    c = m - 6
                    ACT(xr[:, c, 3:515], psb[bk][:, :], AF.Copy, [bank[bk]], [bf("xr%d" % c)])
                else:
                    c = m - 10
                    ACT(gg[:, c, :], psb[bk][:, :], AF.Gelu_apprx_tanh, [bank[bk]], [bf("gg%d" % c)])
                bfree(bk)
                yield
            bk = balloc()
            vblocks = [st1.blk() for _ in range(8)]
            for _ in range(8):
                st1.blk()
            for blk in range(4):
                for kc in range(8):
                    w, wb = vblocks[kc]
                    MM(psb[bk][:, blk * 128:(blk + 1) * 128], hT[:, kc, blk * 128:(blk + 1) * 128], w,
                       kc == 0, kc == 7, [wb, bhT], [bank[bk]], inc=(kc == 7))
            CP("dve", Vt[:, 128:640], psb[bk][:, :], [bank[bk]], [bf("Vt")])
            bfree(bk)
            yield

        def attn_a(s, Q, g):
            Qg = 4 * s + Q
            kbs = [1] if Qg == 0 else [0, 1]
            bq, bk_ = bf("qT"), bf("kT")
            sc = [balloc(), balloc()]
            for hf in range(2):
                n_mm = len(kbs) * 2
                i_mm = 0
                for kb in kbs:
                    for h2 in range(2):
                        hh = h2 * 2 + hf
                        chunk = 2 * g + hh // 2
                        i_mm += 1
                        MM(psb[sc[hf]][:, (kb * 2 + h2) * 128:(kb * 2 + h2 + 1) * 128],
                           kT[hf * 64:(hf + 1) * 64, g, (Q + kb) * 128:(Q + kb + 1) * 128],
                           qT[hf * 64:(hf + 1) * 64, chunk, Q * 128:(Q + 1) * 128],
                           True, True, [bq, bk_], [bank[sc[hf]]], inc=(i_mm == n_mm))
            r = nxt("exp")
            bex = bf("expT%d" % r)
            c0 = 256 if Qg == 0 else 0
            for hf in range(2):
                ACT(expT[:, r, hf * 512 + c0:(hf + 1) * 512], psb[sc[hf]][:, c0:512], AF.Exp, [bank[sc[hf]]], [bex])
                bfree(sc[hf])
            for hf in range(2):
                TT("dve", expT[:, r, hf * 512 + c0:(hf + 1) * 512], expT[:, r, hf * 512 + c0:(hf + 1) * 512],
                   maskt[:, hf * 512 + c0:(hf + 1) * 512], ALU.mult, [bex, bf("mask")], [bex])
            return (r, kbs)

        def attn_b(s, Q, g, state):
            r, kbs = state
            bv, bmix, bex = bf("Vt"), bf("mixT"), bf("expT%d" % r)
            pvb = balloc()
            for hh in range(4):
                cc = hh // 2
                hf = hh % 2
                for which in range(2):
                    col = which * 256 + cc * 128
                    for ki, kb in enumerate(kbs):
                        lhsT = (Vt[:, (Q + kb) * 128 + g * 64:(Q + kb) * 128 + g * 64 + 64]
                                if which == 0 else ones64[:, :])
                        last = (hh == 3 and which == 1 and ki == len(kbs) - 1)
                        ecol = hf * 512 + (kb * 2 + cc) * 128
                        MM(psb[pvb][hf * 64:(hf + 1) * 64, col:col + 128], lhsT,
                           expT[:, r, ecol:ecol + 128],
                           ki == 0, ki == len(kbs) - 1,
                           [bv, bex, bf("ones64")], [bank[pvb]], inc=last)
            rd = nxt("rnn")
            brd = bf("rden%d" % rd)
            for cc in range(2):
                chunk = 2 * g + cc
                TS("dve", rden[:, rd, cc * 128:(cc + 1) * 128], psb[pvb][:, 256 + cc * 128:256 + (cc + 1) * 128],
                   dv[:, 16 + chunk:17 + chunk], None, ALU.add, ..., [bank[pvb], bdv], [brd])
                S.op("dve", lambda e, cc=cc: e.reciprocal(rden[:, rd, cc * 128:(cc + 1) * 128],
                                                          rden[:, rd, cc * 128:(cc + 1) * 128]), [brd], [brd])
                TT("dve", mixT[:, chunk, Q * 128:(Q + 1) * 128], psb[pvb][:, cc * 128:(cc + 1) * 128],
                   rden[:, rd, cc * 128:(cc + 1) * 128], ALU.mult, [bank[pvb], brd], [bmix])
            bfree(pvb)

        def rnn_a(s, c):
            r = nxt("g")
            bxr, bxc, bxcb = bf("xr%d" % c), bf("xc%d" % r), bf("xcb%d" % r)
            TS("dve", xc[:, r, :], xr[:, c, 0:512], pv[:, 0 * 4 + c:0 * 4 + c + 1], pv[:, 16 + c:17 + c],
               ALU.mult, ALU.add, [bxr, bpv], [bxc])
            for k in range(1, 4):
                STT("dve", xc[:, r, :], xr[:, c, k:k + 512], pv[:, k * 4 + c:k * 4 + c + 1], xc[:, r, :],
                    ALU.mult, ALU.add, [bxr, bpv, bxc], [bxc])
            CP("dve", xr[:, c, 0:3], xr[:, c, 512:515], [bxr], [bxr])
            ACT(xcb[:, r, :], xc[:, r, :], AF.Copy, [bxc], [bxcb])
            return r

        def rnn_b(s, c, r):
            bxcb = bf("xcb%d" % r)
            brr, bii, baa = bf("rr%d" % r), bf("ii%d" % r), bf("aa%d" % r)
            ba_, bx_ = balloc(), balloc()
            MM(psb[ba_][:, :], gw[:, (c * 2) * 128:(c * 2 + 1) * 128], xcb[:, r, :], True, True,
               [bf("gw"), bxcb], [bank[ba_]], inc=True)
            MM(psb[bx_][:, :], gw[:, (c * 2 + 1) * 128:(c * 2 + 2) * 128], xcb[:, r, :], True, True,
               [bf("gw"), bxcb], [bank[bx_]], inc=True)
            ACT(rr[:, r, :], psb[ba_][:, :], AF.Tanh, [bank[ba_], bdv], [brr], bias=dv[:, c:c + 1], scale=0.5)
            ACT(ii[:, r, :], psb[bx_][:, :], AF.Tanh, [bank[bx_], bdv], [bii], bias=dv[:, 4 + c:5 + c], scale=0.5)
            bfree(ba_)
            bfree(bx_)
            ACT(aa[:, r, :], rr[:, r, :], AF.Exp, [brr, bdv], [baa], bias=dv[:, 12 + c:13 + c],
                scale=dv[:, 12 + c:13 + c])
            ACT(rr[:, r, :], rr[:, r, :], AF.Exp, [brr, bdv], [brr], bias=dv[:, 8 + c:9 + c],
                scale=dv[:, 8 + c:9 + c])
            ACT(rr[:, r, :], rr[:, r, :], AF.Sqrt, [brr, bf("cst")], [brr], bias=cst[:, 2:3], scale=-0.25)

        def rnn_c(s, c, r):
            bxc = bf("xc%d" % r)
            brr, bii, baa, bhs = bf("rr%d" % r), bf("ii%d" % r), bf("aa%d" % r), bf("hs%d" % r)
            bhst, bmix = bf("hst"), bf("mixT")
            STT("dve", ii[:, r, :], ii[:, r, :], 1.0, xc[:, r, :], ALU.add, ALU.mult, [bii, bxc], [bii])
            TT("dve", rr[:, r, :], rr[:, r, :], ii[:, r, :], ALU.mult, [brr, bii], [brr])
            S.op("dve", lambda e: e.tensor_tensor_scan(hs[:, r, :], aa[:, r, :], rr[:, r, :], hst[:, c:c + 1],
                                                       ALU.mult, ALU.add),
                 [baa, brr, bhst], [bhs])
            CP("dve", hst[:, c:c + 1], hs[:, r, 511:512], [bhs], [bhst])
            TT("dve", mixT[:, 4 + c, :], hs[:, r, :], gg[:, c, :], ALU.mult, [bhs, bf("gg%d" % c)], [bmix])

        def MA_b(s):
            xload(s + 1, 0)
            xload(s + 1, 1)
            for Q in range(4):
                r = rnn_a(s, Q)
                yield
                sa0 = attn_a(s, Q, 0)
                yield
                sa1 = attn_a(s, Q, 1)
                yield
                rnn_b(s, Q, r)
                yield
                attn_b(s, Q, 0, sa0)
                yield
                attn_b(s, Q, 1, sa1)
                yield
                rnn_c(s, Q, r)
                yield
            CP("dve", kT[:, :, 0:128], kT[:, :, 512:640], [bf("kT")], [bf("kT")])
            CP("dve", Vt[:, 0:128], Vt[:, 512:640], [bf("Vt")], [bf("Vt")])
            yield

        def ln_stats_prep(zsrc, bz):
            r = nxt("z")
            bzb, bzq = bf("zb%d" % r), bf("zq%d" % r)
            ACT(zb[:, r, :], zsrc, AF.Copy, [bz], [bzb])
            ACT(zq[:, r, :], zsrc, AF.Square, [bz], [bzq])
            return r

        def ln_stats_mm(st, m, r):
            bzb, bzq = bf("zb%d" % r), bf("zq%d" % r)
            MM(psb[st[0]][:, :], onesm[:, :], zb[:, r, :], m == 0, m == 7, [bf("onesm"), bzb], [bank[st[0]]],
               inc=(m == 7))
            MM(psb[st[1]][:, :], onesm[:, :], zq[:, r, :], m == 0, m == 7, [bf("onesm"), bzq], [bank[st[1]]],
               inc=(m == 7))

        def ln_finish(st):
            bl = bf("lnm")
            ACT(lnm[:, 2, :], psb[st[0]][:, :], AF.Copy, [bank[st[0]]], [bl])
            TT("dve", lnm[:, 0, :], lnm[:, 2, :], lnm[:, 2, :], ALU.mult, [bl], [bl])
            STT("dve", lnm[:, 0, :], lnm[:, 0, :], -1.0, psb[st[1]][:, :], ALU.mult, ALU.add,
                [bl, bank[st[1]]], [bl])
            ACT(lnm[:, 0, :], lnm[:, 0, :], AF.Sqrt, [bl, bf("cst")], [bl], bias=cst[:, 3:4], scale=1.0)
            S.op("dve", lambda e: e.reciprocal(lnm[:, 0, :], lnm[:, 0, :]), [bl], [bl])
            STT("dve", lnm[:, 1, :], lnm[:, 2, :], -1.0, lnm[:, 0, :], ALU.mult, ALU.mult, [bl], [bl])
            bfree(st[0])
            bfree(st[1])

        def MC(s):
            xb = s % 2
            bxT = [bf("xT%d_%d" % (xb, c)) for c in range(8)]
            bmix, bl, bh1 = bf("mixT"), bf("lnm"), bf("h1T")
            pload(s)
            st = (balloc(), balloc())
            for m in range(8):
                bk = balloc()
                for kc in range(8):
                    w, wb = st2.blk()
                    MM(psb[bk][:, :], w, mixT[:, kc, :], kc == 0, kc == 7, [wb, bmix], [bank[bk]], inc=(kc == 7))
                STT("dve", xT[:, xb, m, :], xT[:, xb, m, :], ALPHA, psb[bk][:, :], ALU.mult, ALU.add,
                    [bxT[m], bank[bk]], [bxT[m]])
                bfree(bk)
                if m > 0:
                    ln_stats_mm(st, m - 1, rprev)
                rprev = ln_stats_prep(xT[:, xb, m, :], bxT[m])
                yield
            ln_stats_mm(st, 7, rprev)
            ln_finish(st)
            yield
            for m in range(8):
                TT("dve", xT[:, xb, m, :], xT[:, xb, m, :], lnm[:, 0, :], ALU.mult, [bxT[m], bl], [bxT[m]])
                TT("dve", xT[:, xb, m, :], xT[:, xb, m, :], lnm[:, 1, :], ALU.add, [bxT[m], bl], [bxT[m]])
                ACT(h1T[:, m, :], xT[:, xb, m, :], AF.Identity, [bxT[m], bpv], [bh1],
                    bias=pv[:, 144 + m:145 + m], scale=pv[:, 136 + m:137 + m])
                ACT(xT[:, xb, m, :], xT[:, xb, m, :], AF.Identity, [bxT[m], bdv, bf("ab1")], [bxT[m]],
                    bias=ab1[:, m:m + 1], scale=dv[:, 28 + m:29 + m])
                yield

        def F_up(s):
            bh1, bact, bhalo = bf("h1T"), bf("actb"), bf("halo")
            if s == 0:
                for k in range(6):
                    bact.r.extend(bso[k].r)
                    if bso[k].w is not None:
                        bact.r.append(bso[k].w)

            def stage_b(j, ry, bv_):
                byy = bf("yy%d" % ry)
                ACT(yy[:, ry, :], yy[:, ry, :], AF.Gelu_apprx_tanh, [byy], [byy])
                TT("dve", actb[:, j, :], yy[:, ry, :], psb[bv_][:, :], ALU.mult, [byy, bank[bv_]], [bact])
                bfree(bv_)

            pend = None
            for j in range(24):
                bg_, bv_ = balloc(), balloc()
                for kc in range(8):
                    w, wb = st2.blk()
                    MM(psb[bg_][:, :], w, h1T[:, kc, :], kc == 0, kc == 7, [wb, bh1], [bank[bg_]], inc=(kc == 7))
                for kc in range(8):
                    w, wb = st2.blk()
                    MM(psb[bv_][:, :], w, h1T[:, kc, :], kc == 0, kc == 7, [wb, bh1], [bank[bv_]], inc=(kc == 7))
                r = nxt("g")
                ry = rot3[0]
                rot3[0] = (ry + 1) % 3
                bgr, byy = bf("graw%d" % r), bf("yy%d" % ry)
                CP("dve", graw[:, r, 0:2], halo[:, j, :], [bhalo], [bgr])
                ACT(graw[:, r, 2:514], psb[bg_][:, :], AF.Copy, [bank[bg_]], [bgr])
                ACT(halo[:, j, :], psb[bg_][:, 510:512], AF.Copy, [bank[bg_]], [bhalo])
                ACT(yy[:, ry, :], psb[bg_][:, :], AF.Identity, [bank[bg_], bpv], [byy],
                    bias=pv[:, 104 + j:105 + j], scale=pv[:, 32 + 2 * 24 + j:33 + 2 * 24 + j])
                bfree(bg_)
                for k in range(0, 2):
                    STT("dve", yy[:, ry, :], graw[:, r, k:k + 512], pv[:, 32 + k * 24 + j:33 + k * 24 + j],
                        yy[:, ry, :], ALU.mult, ALU.add, [bgr, bpv, byy], [byy])
                if pend is not None:
                    stage_b(*pend)
                pend = (j, ry, bv_)
                yield
            stage_b(*pend)
            yield

        def F_down(s):
            t0 = s * T
            xb = s % 2
            bxT = [bf("xT%d_%d" % (xb, c)) for c in range(8)]
            bh1, bact, bl, bpr, bpT = bf("h1T"), bf("actb"), bf("lnm"), bf("praw"), bf("pT")
            pload(s)
            for k2 in range(2):
                bk = balloc()
                for blk in range(4):
                    TR(psb[bk][:, blk * 128:(blk + 1) * 128], praw[:, blk, k2 * 128:(k2 + 1) * 128],
                       [bpr], [bank[bk]], inc=(blk == 3))
                CP("dve", pT[:, k2, :], psb[bk][:, :], [bank[bk]], [bpT])
                bfree(bk)
            yield
            for m in range(8):
                bg_, bp_ = balloc(), balloc()
                for kc in range(8):
                    w, wb = st2.blk()
                    MM(psb[bg_][:, :], w, h1T[:, kc, :], kc == 0, kc == 7, [wb, bh1], [bank[bg_]], inc=(kc == 7))
                for k2 in range(2):
                    w, wb = st2.blk()
                    MM(psb[bp_][:, :], w, pT[:, k2, :], k2 == 0, k2 == 1, [wb, bpT], [bank[bp_]], inc=(k2 == 1))
                r = nxt("sg")
                bsg = bf("sg%d" % r)
                ACT(sg[:, r, :], psb[bg_][:, :], AF.Tanh, [bank[bg_], bdv], [bsg], bias=dv[:, 20 + m:21 + m],
                    scale=0.5)
                bfree(bg_)
                STT("dve", sg[:, r, :], sg[:, r, :], 1.0, psb[bp_][:, :], ALU.add, ALU.mult,
                    [bsg, bank[bp_]], [bsg])
                bfree(bp_)
                STT("dve", xT[:, xb, m, :], sg[:, r, :], 0.5, xT[:, xb, m, :], ALU.mult, ALU.add,
                    [bsg, bxT[m]], [bxT[m]])
                yield
            st = (balloc(), balloc())
            for m in range(8):
                bk = balloc()
                for c in range(24):
                    w, wb = st2.blk()
                    MM(psb[bk][:, :], w, actb[:, c, :], c == 0, c == 23, [wb, bact], [bank[bk]], inc=(c == 23))
                TT("dve", xT[:, xb, m, :], xT[:, xb, m, :], psb[bk][:, :], ALU.add, [bxT[m], bank[bk]], [bxT[m]])
                bfree(bk)
                if m > 0:
                    ln_stats_mm(st, m - 1, rprev)
                rprev = ln_stats_prep(xT[:, xb, m, :], bxT[m])
                yield
            ln_stats_mm(st, 7, rprev)
            ln_finish(st)
            yield
            for m in range(8):
                TT("dve", xT[:, xb, m, :], xT[:, xb, m, :], lnm[:, 0, :], ALU.mult, [bxT[m], bl], [bxT[m]])
                TT("dve", xT[:, xb, m, :], xT[:, xb, m, :], lnm[:, 1, :], ALU.add, [bxT[m], bl], [bxT[m]])
                ACT(xT[:, xb, m, :], xT[:, xb, m, :], AF.Identity, [bxT[m], bpv], [bxT[m]],
                    bias=pv[:, 160 + m:161 + m], scale=pv[:, 152 + m:153 + m])
                yield
            for blk in range(4):
                r = 0
                ro = nxt("ot")
                bot = bf("ot%d" % r)
                for half in range(2):
                    bk = balloc()
                    for k4 in range(4):
                        m = half * 4 + k4
                        TR(psb[bk][:, k4 * 128:(k4 + 1) * 128], xT[:, xb, m, blk * 128:(blk + 1) * 128],
                           [bxT[m]], [bank[bk]], inc=(k4 == 3))
                    if half == 0:
                        ACT(ot[:, r, 0:512], psb[bk][:, :], AF.Copy, [bank[bk]], [bot])
                    else:
                        CP("dve", ot[:, r, 512:1024], psb[bk][:, :], [bank[bk]], [bot])
                    bfree(bk)
                dst = out_d[t0 + blk * 128:t0 + (blk + 1) * 128, :]
                ev = S.dma("sp", sem_o[ro], lambda e, r=r, dst=dst: e.dma_start(out=dst, in_=ot[:, r, :]),
                           reads=[bot], writes=[bf("outd%d" % ro)])
                out_evs.append(ev)
                yield

        if dbg == 0:
            cg = cast_gen()
            for s in range(ntiles + 1):
                g1 = MA_a(s) if s < ntiles else None
                g2 = F_up(s - 1) if s >= 1 else None
                if s == 0:
                    interleave(g1, take(cg, 22))
                else:
                    interleave(g1, g2)
                g3 = MA_b(s) if s < ntiles else None
                mc = MC(s) if s < ntiles else None
                if s == 0:
                    interleave(g3, cg)
                    interleave(mc)
                else:
                    fd = F_down(s - 1)
                    interleave_pat(g3, take(fd, 17), "AAB" * 15)
                    interleave(g3)
                    interleave(fd, mc)
            assert st1.pos == 128 * ntiles and st2.pos == 720 * ntiles, (st1.pos, st2.pos)
        else:
            interleave(cast_gen())
            stages = [MA_a, MA_b, MC, F_up, F_down]
            for st_ in stages[:dbg - 1]:
                interleave(st_(0))
        S.wait_events("sp", [e_ for e_ in out_evs[-2:] if e_ is not None] + [S.last[k] for k in sem_so if k in S.last])
        if dbg or limit:
            S.drain("sp")
        S.emit()
    return nc


def _blocks_to_pieces(blocks):
    n = len(blocks)
    assert n % 16 == 0
    arr = np.stack(blocks, 0).reshape(n // 16, 16, 128, 128)
    return np.ascontiguousarray(arr.transpose(0, 2, 1, 3).reshape(n // 16, 128, 2048))


def prep_weights(w_in, w_out, w_ffn_up, w_ffn_down, ple_gate_w, ple_proj):
    w_in, w_out, w_up, w_dn, wg, wp = (np.asarray(a[0], np.float32) for a in
                                       (w_in, w_out, w_ffn_up, w_ffn_down, ple_gate_w, ple_proj))
    cols = [w_in[:, 0:512],
            w_in[:, 512:576], w_in[:, 512:576],
            w_in[:, 576:640], w_in[:, 576:640],
            w_in[:, 768:1280], w_in[:, 1280:1792], w_in[:, 640:768],
            np.zeros((1024, 128), np.float32)]
    wi = np.concatenate(cols, axis=1)
    assert wi.shape == (1024, 2048)
    b1 = [wi[kc * 128:(kc + 1) * 128, m * 128:(m + 1) * 128] for m in range(16) for kc in range(8)]
    w1 = _blocks_to_pieces(b1)
    b2 = []
    for m in range(8):
        for kc in range(8):
            b2.append(w_out[kc * 128:(kc + 1) * 128, m * 128:(m + 1) * 128])
    for j in range(24):
        for kc in range(8):
            b2.append(w_up[kc * 128:(kc + 1) * 128, j * 128:(j + 1) * 128])
        for kc in range(8):
            b2.append(w_up[kc * 128:(kc + 1) * 128, 3072 + j * 128:3072 + (j + 1) * 128])
    for m in range(8):
        for kc in range(8):
            b2.append(wg[kc * 128:(kc + 1) * 128, m * 128:(m + 1) * 128])
        for k2 in range(2):
            b2.append(wp[k2 * 128:(k2 + 1) * 128, m * 128:(m + 1) * 128])
    for m in range(8):
        for c in range(24):
            b2.append(w_dn[c * 128:(c + 1) * 128, m * 128:(m + 1) * 128])
    w2 = _blocks_to_pieces(b2)
    assert w1.shape[0] == NP1 and w2.shape[0] == NP2
    return w1, w2


def prep_small(attn_sinks, rnn_conv_w, rnn_conv_b, gate_a_w, gate_a_b, gate_x_w, gate_x_b, lru_lambda,
               ln1_g, ln1_b, ffn_conv_w, ffn_conv_b, ple_gate_b, ln2_g, ln2_b):
    f = lambda a: np.asarray(a[0], np.float32)
    pv = np.zeros((128, NPV), np.float32)
    cw = f(rnn_conv_w)
    for k in range(4):
        pv[:, k * 4:(k + 1) * 4] = cw[k].reshape(4, 128).T
    pv[:, 16:20] = f(rnn_conv_b).reshape(4, 128).T
    pv[:, 20:24] = f(gate_a_b).reshape(4, 128).T
    pv[:, 24:28] = f(gate_x_b).reshape(4, 128).T
    pv[:, 28:32] = f(lru_lambda).reshape(4, 128).T
    fw = f(ffn_conv_w)
    for k in range(3):
        pv[:, 32 + k * 24:32 + (k + 1) * 24] = fw[k].reshape(24, 128).T
    pv[:, 104:128] = f(ffn_conv_b).reshape(24, 128).T
    pv[:, 128:136] = f(ple_gate_b).reshape(8, 128).T
    pv[:, 136:144] = f(ln1_g).reshape(8, 128).T
    pv[:, 144:152] = f(ln1_b).reshape(8, 128).T
    pv[:, 152:160] = f(ln2_g).reshape(8, 128).T
    pv[:, 160:168] = f(ln2_b).reshape(8, 128).T
    sk = f(attn_sinks)
    for m in range(4):
        pv[0:64, 168 + m] = sk[2 * m]
        pv[64:128, 168 + m] = sk[2 * m + 1]
    gw = np.zeros((128, 1024), np.float32)
    for c in range(4):
        for gi, wsrc in enumerate((f(gate_a_w), f(gate_x_w))):
            blk = np.zeros((128, 128), np.float32)
            blk[0:64, 0:64] = wsrc[2 * c]
            blk[64:128, 64:128] = wsrc[2 * c + 1]
            gw[:, (c * 2 + gi) * 128:(c * 2 + gi + 1) * 128] = blk
    s_idx = np.arange(128)[:, None]
    q_idx = np.arange(128)[None, :]
    mprev = (s_idx > q_idx).astype(np.float32)
    mcur = (s_idx <= q_idx).astype(np.float32)
    mask = np.concatenate([mprev, mprev, mcur, mcur, mprev, mprev, mcur, mcur], axis=1)
    ident = np.eye(128, dtype=np.float32)
    return pv, gw, np.ascontiguousarray(mask), ident


_NC_CACHE = {}


def run(inputs, NT, trace=False):
    x = np.asarray(inputs["x"], np.float32)
    p = np.asarray(inputs["p"], np.float32)[0]
    w1, w2 = prep_weights(inputs["w_in"], inputs["w_out"], inputs["w_ffn_up"], inputs["w_ffn_down"],
                          inputs["ple_gate_w"], inputs["ple_proj"])
    pv, gw, mask, ident = prep_small(inputs["attn_sinks"], inputs["rnn_conv_w"], inputs["rnn_conv_b"],
                                     inputs["gate_a_w"], inputs["gate_a_b"], inputs["gate_x_w"],
                                     inputs["gate_x_b"], inputs["lru_lambda"], inputs["ln1_g"], inputs["ln1_b"],
                                     inputs["ffn_conv_w"], inputs["ffn_conv_b"], inputs["ple_gate_b"],
                                     inputs["ln2_g"], inputs["ln2_b"])
    if NT not in _NC_CACHE:
        _NC_CACHE[NT] = build(NT)
    nc = _NC_CACHE[NT]
    in_maps = []
    for b in range(NB):
        in_maps.append({"x": np.ascontiguousarray(x[b, :NT]), "p": np.ascontiguousarray(p[b, :NT]),
                        "w1": w1, "w2": w2, "gw": gw, "mask": mask, "ident": ident, "pv": pv})
    res = run_bass_kernel_spmd(nc, in_maps, core_ids=list(range(NB)), trace=trace)
    out = np.stack([r["out"] for r in res.results], 0)
    return out, res


def kernel(**inputs):
    out, _ = run(inputs, SEQ)
    return out.astype(np.float32)
```
